# Optimizing a Trainium2 kernel written in Bass

```python
import jax
import jax.numpy as jnp
from jax import lax
import numpy as np

D_MODEL = 4096
BATCH = 4
SEQ = 2048
DEPTH = 2

HEAD_DIM = 128
D_MIX = D_MODEL
D_RG = D_MIX // 4
D_NSA = D_MIX // 2
D_HG = D_MIX - D_RG - D_NSA

RG_BLOCKS = D_RG // HEAD_DIM
RG_CONV = 4
RG_C = 8.0

NSA_HEADS = D_NSA // HEAD_DIM
NSA_KV = 2
NSA_GROUP = NSA_HEADS // NSA_KV
CMP_LEN = 32
CMP_STRIDE = 16
CMP_HIDDEN = 256
SEL_LEN = 64
SEL_TOPN = 16
WINDOW = 512
Q_BLOCK = 64

HG_HEADS = D_HG // HEAD_DIM
HG_CHUNK = 64

ALPHA = (2.0 * DEPTH) ** 0.25
BETA = (8.0 * DEPTH) ** -0.25
ADA_SCALE = 0.1
LN_EPS = 1e-5
RMS_EPS = 1e-6
NEG_INF = -1e30
FORCE_SCORE = 1e9

KV_COLS = 2 * NSA_KV * HEAD_DIM
IN_SIZES = (D_RG, D_RG,
            D_NSA, KV_COLS, KV_COLS, KV_COLS, 3 * NSA_HEADS, D_NSA,
            D_HG, D_HG, D_HG, D_HG)
N_IN = sum(IN_SIZES)

kernel_name = "hymba_rglru_nsa_hgrn2_deepnorm"


def _layer_norm(x, g, b):
    xf = x.astype(jnp.float32)
    mu = jnp.mean(xf, axis=-1, keepdims=True)
    var = jnp.mean(jnp.square(xf - mu), axis=-1, keepdims=True)
    return ((xf - mu) * lax.rsqrt(var + LN_EPS)).astype(x.dtype) * g + b


def _masked_softmax(logits, valid):
    p = jax.nn.softmax(jnp.where(valid, logits, NEG_INF), axis=-1)
    return p * valid


def _alibi_slopes():
    h = np.arange(1, NSA_HEADS + 1, dtype=np.float32)
    s = np.power(np.float32(2.0), -8.0 * h / NSA_HEADS).astype(np.float32)
    return jnp.asarray(s.reshape(NSA_KV, NSA_GROUP))


def _causal_dwconv(x, w, b):
    T = x.shape[1]
    xp = jnp.pad(x, ((0, 0), (RG_CONV - 1, 0), (0, 0)))
    y = b
    for j in range(RG_CONV):
        y = y + xp[:, j:j + T] * w[j]
    return y


def _rg_lru(x, w_a, b_a, w_x, b_x, lam):
    B, T, _ = x.shape
    xb = x.reshape(B, T, RG_BLOCKS, HEAD_DIM)
    r = jax.nn.sigmoid(jnp.einsum('btnd,nde->btne', xb, w_a).reshape(B, T, D_RG) + b_a)
    i = jax.nn.sigmoid(jnp.einsum('btnd,nde->btne', xb, w_x).reshape(B, T, D_RG) + b_x)
    log_a = (-RG_C * jax.nn.softplus(-lam) * r).astype(jnp.float32)
    a = jnp.exp(log_a)
    mult = jnp.sqrt(-jnp.expm1(2.0 * log_a))
    mult = mult.at[:, 0].set(1.0)
    bx = mult * (i * x).astype(jnp.float32)

    def combine(left, right):
        a_l, b_l = left
        a_r, b_r = right
        return a_l * a_r, a_r * b_l + b_r

    _, h = lax.associative_scan(combine, (a, bx), axis=1)
    return h.astype(x.dtype)


def _nsa(q, kv_c, kv_s, kv_w, gate_logits, pe_k, pe_v, w1_k, w2_k, w1_v, w2_v):
    B, T, _ = q.shape
    dt = q.dtype
    f32 = jnp.float32
    G, R, D = NSA_KV, NSA_GROUP, HEAD_DIM
    scale = D ** -0.5
    slopes = _alibi_slopes()
    qh = q.reshape(B, T, G, R, D)

    def kv_split(kv):
        k, v = jnp.split(kv, 2, axis=-1)
        return k.reshape(B, T, G, D), v.reshape(B, T, G, D)

    kc, vc = kv_split(kv_c)
    ks, vs = kv_split(kv_s)
    kw, vw = kv_split(kv_w)
    t_pos = jnp.arange(T)

    n_cmp = (T - CMP_LEN) // CMP_STRIDE + 1
    cmp_idx = np.arange(n_cmp)[:, None] * CMP_STRIDE + np.arange(CMP_LEN)[None, :]

    def compress(k, pe, w1, w2):
        blk = k[:, cmp_idx] + pe[:, None, :]
        blk = blk.transpose(0, 1, 3, 2, 4).reshape(B, n_cmp, G, CMP_LEN * D)
        return jax.nn.silu(blk @ w1) @ w2

    k_cmp = compress(kc, pe_k, w1_k, w2_k)
    v_cmp = compress(vc, pe_v, w1_v, w2_v)
    cmp_end = jnp.asarray(cmp_idx[:, -1])
    d_cmp = (t_pos[:, None] - cmp_end[None, :]).astype(f32)
    lg_cmp = (jnp.einsum('btgrd,bjgd->bgrtj', qh, k_cmp).astype(f32) * scale
              - slopes[None, :, :, None, None] * d_cmp)
    p_cmp = _masked_softmax(lg_cmp, d_cmp >= 0)
    o_cmp = jnp.einsum('bgrtj,bjgd->btgrd', p_cmp.astype(dt), v_cmp)

    n_sel = T // SEL_LEN
    top_n = min(SEL_TOPN, n_sel)
    s_c = np.arange(n_cmp) * CMP_STRIDE
    s_s = np.arange(n_sel) * SEL_LEN
    overlap = np.clip(np.minimum(s_c[:, None] + CMP_LEN, s_s[None, :] + SEL_LEN)
                      - np.maximum(s_c[:, None], s_s[None, :]), 0, None).astype(np.float32) / CMP_LEN
    imp = jnp.einsum('bgrtj,jn->bgtn', p_cmp, jnp.asarray(overlap))
    n_ids = jnp.arange(n_sel)
    cur = t_pos // SEL_LEN
    future = jnp.asarray(s_s)[None, :] > t_pos[:, None]
    forced = (n_ids[None, :] == 0) | (n_ids[None, :] == cur[:, None]) | (n_ids[None, :] == cur[:, None] - 1)
    imp = jnp.where(future, NEG_INF, jnp.where(forced, FORCE_SCORE, imp))
    _, sel_idx = lax.top_k(imp, top_n)

    n_qb = T // Q_BLOCK
    q_blocks = qh.reshape(B, n_qb, Q_BLOCK, G, R, D).transpose(1, 0, 2, 3, 4, 5)
    idx_blocks = sel_idx.reshape(B, G, n_qb, Q_BLOCK, top_n).transpose(2, 0, 1, 3, 4)
    ks_blk = ks.reshape(B, n_sel, SEL_LEN, G, D).transpose(0, 3, 1, 2, 4)
    vs_blk = vs.reshape(B, n_sel, SEL_LEN, G, D).transpose(0, 3, 1, 2, 4)
    kw_pad = jnp.pad(kw, ((0, 0), (WINDOW, 0), (0, 0), (0, 0)))
    vw_pad = jnp.pad(vw, ((0, 0), (WINDOW, 0), (0, 0), (0, 0)))
    b_ix = jnp.arange(B)[:, None, None, None]
    g_ix = jnp.arange(G)[None, :, None, None]
    n_keys = top_n * SEL_LEN

    def block_fn(args):
        qb, idx, qb_i = args
        q0 = qb_i * Q_BLOCK
        tq = q0 + jnp.arange(Q_BLOCK)
        kg = ks_blk[b_ix, g_ix, idx]
        vg = vs_blk[b_ix, g_ix, idx]
        pos = idx[..., None] * SEL_LEN + jnp.arange(SEL_LEN)
        d_sel = (tq[None, None, :, None, None] - pos).reshape(B, G, 1, Q_BLOCK, n_keys).astype(f32)
        lg = jnp.einsum('bqgrd,bgqnld->bgrqnl', qb, kg).reshape(B, G, R, Q_BLOCK, n_keys).astype(f32) * scale
        p_sel = _masked_softmax(lg - slopes[None, :, :, None, None] * d_sel, d_sel >= 0)
        o_sel = jnp.einsum('bgrqk,bgqkd->bqgrd', p_sel.astype(dt), vg.reshape(B, G, Q_BLOCK, n_keys, D))
        kwin = lax.dynamic_slice_in_dim(kw_pad, q0, WINDOW + Q_BLOCK, axis=1)
        vwin = lax.dynamic_slice_in_dim(vw_pad, q0, WINDOW + Q_BLOCK, axis=1)
        kpos = q0 - WINDOW + jnp.arange(WINDOW + Q_BLOCK)
        d_win = tq[:, None] - kpos[None, :]
        valid = (d_win >= 0) & (d_win < WINDOW) & (kpos[None, :] >= 0)
        lw = (jnp.einsum('bqgrd,bkgd->bgrqk', qb, kwin).astype(f32) * scale
              - slopes[None, :, :, None, None] * d_win.astype(f32))
        p_win = _masked_softmax(lw, valid)
        o_win = jnp.einsum('bgrqk,bkgd->bqgrd', p_win.astype(dt), vwin)
        return o_sel, o_win

    o_sel, o_win = lax.map(block_fn, (q_blocks, idx_blocks, jnp.arange(n_qb)))
    o_sel = o_sel.transpose(1, 0, 2, 3, 4, 5).reshape(B, T, G, R, D)
    o_win = o_win.transpose(1, 0, 2, 3, 4, 5).reshape(B, T, G, R, D)

    g = jax.nn.sigmoid(gate_logits.reshape(B, T, G, R, 3))
    o = g[..., 0:1] * o_cmp + g[..., 1:2] * o_sel + g[..., 2:3] * o_win
    return o.reshape(B, T, D_NSA)


def _hgrn2(q, f_logit, v, out_gate, lb, norm_g):
    B, T, _ = q.shape
    dt = q.dtype
    f32 = jnp.float32
    n_ch = T // HG_CHUNK
    z = f_logit.astype(f32)
    lb = lb.astype(f32)
    log_f = jnp.logaddexp(jnp.log(lb), jnp.log1p(-lb) + jax.nn.log_sigmoid(z))
    k = (1.0 - lb) * jax.nn.sigmoid(-z)
    qs = jax.nn.silu(q.astype(f32))

    def chunks(a):
        return a.reshape(B, n_ch, HG_CHUNK, HG_HEADS, HEAD_DIM).transpose(1, 0, 2, 3, 4)

    qc, kc, vc, lfc = chunks(qs), chunks(k), chunks(v.astype(f32)), chunks(log_f)
    causal = jnp.tril(jnp.ones((HG_CHUNK, HG_CHUNK), dtype=bool))

    def step(S, inp):
        q_, k_, v_, lf = inp
        b = jnp.cumsum(lf, axis=1)
        b_last = b[:, -1]
        o_inter = jnp.einsum('bthk,bhkv->bthv', q_ * jnp.exp(b), S)
        diff = b[:, :, None] - b[:, None, :]
        decay = jnp.exp(jnp.where(causal[None, :, :, None, None], diff, -jnp.inf))
        A = jnp.einsum('bthk,btshk->bhts', q_, decay * k_[:, None])
        o_intra = jnp.einsum('bhts,bshv->bthv', A, v_)
        S = S * jnp.exp(b_last)[..., None] + jnp.einsum('bshk,bshv->bhkv', k_ * jnp.exp(b_last[:, None] - b), v_)
        return S, o_inter + o_intra

    S0 = jnp.zeros((B, HG_HEADS, HEAD_DIM, HEAD_DIM), f32)
    _, o = lax.scan(step, S0, (qc, kc, vc, lfc))
    o = o.transpose(1, 0, 2, 3, 4).reshape(B, T, HG_HEADS, HEAD_DIM)
    o = o * lax.rsqrt(jnp.mean(o * o, axis=-1, keepdims=True) + RMS_EPS) * norm_g.astype(f32)
    return (o.reshape(B, T, D_HG) * jax.nn.silu(out_gate.astype(f32))).astype(dt)


def setup_inputs(seed: int = 0) -> dict:
    key = jax.random.key(seed)
    ks = jax.random.split(key, 24)
    f32 = jnp.float32

    def nrm(k, shape, s):
        return jax.random.normal(k, shape, f32) * s

    a0 = jax.random.uniform(ks[11], (DEPTH, D_RG), f32, 0.9, 0.999) ** (1.0 / RG_C)
    return {
        "x": nrm(ks[0], (BATCH, SEQ, D_MODEL), 1.0),
        "c": nrm(ks[1], (BATCH, D_MODEL), 1.0),
        "w_ada": nrm(ks[2], (DEPTH, D_MODEL, 3 * D_MODEL), ADA_SCALE * D_MODEL ** -0.5),
        "b_ada": nrm(ks[3], (DEPTH, 3 * D_MODEL), 0.02),
        "w_in": nrm(ks[4], (DEPTH, D_MODEL, N_IN), D_MODEL ** -0.5),
        "rg_conv_w": nrm(ks[5], (DEPTH, RG_CONV, D_RG), RG_CONV ** -0.5),
        "rg_conv_b": nrm(ks[6], (DEPTH, D_RG), 0.02),
        "rg_w_a": nrm(ks[7], (DEPTH, RG_BLOCKS, HEAD_DIM, HEAD_DIM), HEAD_DIM ** -0.5),
        "rg_b_a": nrm(ks[8], (DEPTH, D_RG), 0.02),
        "rg_w_x": nrm(ks[9], (DEPTH, RG_BLOCKS, HEAD_DIM, HEAD_DIM), HEAD_DIM ** -0.5),
        "rg_b_x": nrm(ks[10], (DEPTH, D_RG), 0.02),
        "rg_lambda": jnp.log(a0) - jnp.log1p(-a0),
        "nsa_pe_k": nrm(ks[12], (DEPTH, CMP_LEN, HEAD_DIM), 0.1),
        "nsa_pe_v": nrm(ks[13], (DEPTH, CMP_LEN, HEAD_DIM), 0.1),
        "nsa_cmp_w1_k": nrm(ks[14], (DEPTH, CMP_LEN * HEAD_DIM, CMP_HIDDEN), (CMP_LEN * HEAD_DIM) ** -0.5),
        "nsa_cmp_w2_k": nrm(ks[15], (DEPTH, CMP_HIDDEN, HEAD_DIM), CMP_HIDDEN ** -0.5),
        "nsa_cmp_w1_v": nrm(ks[16], (DEPTH, CMP_LEN * HEAD_DIM, CMP_HIDDEN), (CMP_LEN * HEAD_DIM) ** -0.5),
        "nsa_cmp_w2_v": nrm(ks[17], (DEPTH, CMP_HIDDEN, HEAD_DIM), CMP_HIDDEN ** -0.5),
        "hg_lower_bounds": nrm(ks[18], (DEPTH, D_HG), 1.0),
        "hg_norm_g": 1.0 + nrm(ks[19], (DEPTH, HEAD_DIM), 0.02),
        "w_out": nrm(ks[20], (DEPTH, D_MIX, D_MODEL), BETA * D_MIX ** -0.5),
        "ln_g": 1.0 + nrm(ks[21], (DEPTH, D_MODEL), 0.02),
        "ln_b": nrm(ks[22], (DEPTH, D_MODEL), 0.02),
    }


def reference(x, c, w_ada, b_ada, w_in, rg_conv_w, rg_conv_b, rg_w_a, rg_b_a, rg_w_x, rg_b_x,
              rg_lambda, nsa_pe_k, nsa_pe_v, nsa_cmp_w1_k, nsa_cmp_w2_k, nsa_cmp_w1_v, nsa_cmp_w2_v,
              hg_lower_bounds, hg_norm_g, w_out, ln_g, ln_b):
    lbs = jnp.cumsum(jax.nn.softmax(hg_lower_bounds.astype(jnp.float32), axis=0), axis=0)
    lbs = lbs - lbs[0]
    splits = [int(s) for s in np.cumsum(IN_SIZES)[:-1]]
    for l in range(DEPTH):
        mod = c @ w_ada[l] + b_ada[l]
        shift, scale, gate = jnp.split(mod, 3, axis=-1)
        u = x * (1.0 + scale[:, None]) + shift[:, None]
        proj = u @ w_in[l]
        (rg_x, rg_g, nsa_q, kv_c, kv_s, kv_w, nsa_gl, nsa_g,
         hg_q, hg_f, hg_i, hg_g) = jnp.split(proj, splits, axis=-1)
        y_rg = _rg_lru(_causal_dwconv(rg_x, rg_conv_w[l], rg_conv_b[l]),
                       rg_w_a[l], rg_b_a[l], rg_w_x[l], rg_b_x[l], rg_lambda[l]) * jax.nn.silu(rg_g)
        y_nsa = _nsa(nsa_q, kv_c, kv_s, kv_w, nsa_gl, nsa_pe_k[l], nsa_pe_v[l],
                     nsa_cmp_w1_k[l], nsa_cmp_w2_k[l], nsa_cmp_w1_v[l], nsa_cmp_w2_v[l]) * jax.nn.silu(nsa_g)
        y_hg = _hgrn2(hg_q, hg_f, hg_i, hg_g, lbs[l], hg_norm_g[l])
        y = jnp.concatenate([y_rg, y_nsa, y_hg], axis=-1) @ w_out[l]
        x = _layer_norm(ALPHA * x + (1.0 + gate[:, None]) * y, ln_g[l], ln_b[l])
    return x
```

```python
from contextlib import ExitStack
import numpy as np
import ml_dtypes
import concourse.bass as bass
import concourse.mybir as mybir
from concourse.bass_utils import run_bass_kernel_spmd

F32 = mybir.dt.float32
BF16 = mybir.dt.bfloat16
I32 = mybir.dt.int32
AF = mybir.ActivationFunctionType
ALU = mybir.AluOpType
AX = mybir.AxisListType

T = 2048
D = 4096
DEPTH = 2
NKC = 32
NCT_IN = 93
NPROJ = NCT_IN * 128
OFF = dict(rgx=0, rgg=1024, q=2048, kvc=4096, kvs=4608, kvw=5120, nsag=5632,
           hq=7680, hf=8704, hi=9728, hg=10752, gl=11776)
ALPHA = (2.0 * DEPTH) ** 0.25
LN_EPS = 1e-5
RMS_EPS = 1e-6
SCALE = 128 ** -0.5
NEG = -30000.0
BIGS = 30000.0 / SCALE
P_CONVW, P_CONVB, P_BA, P_BX, P_LAM, P_LB, P_NG, P_LNG, P_LNB, P_BADA, P_PEK, P_PEV = (
    0, 32, 40, 48, 56, 64, 80, 81, 113, 145, 241, 273)
NPRM = 305

ENGS = ("pe", "dve", "act", "pool", "sp")
SEM_ROLL = 30000
N_DMA_SEMS = 40


class Buf:
    __slots__ = ("name", "w", "r")

    def __init__(self, name=""):
        self.name = name
        self.w = None
        self.r = {}


class KB:
    def __init__(self, nc, es):
        self.nc = nc
        self.es = es
        self.les = es
        self.q = {e: [] for e in ENGS}
        self.eng_sem = {}
        self.eng_cnt = {}
        self.eng_last = {}
        self.waited = {e: {} for e in ENGS}
        self.semn = 0
        self.dma_sems = []
        self.dma_cnt = []
        self.dma_last = []
        self.dma_i = 0
        self.nalloc = 0
        self.yield_hook = None

    def newsem(self, name):
        self.semn += 1
        return self.es.enter_context(self.nc.semaphore(f"{name}_{self.semn}"))

    def sb(self, name, shape, dtype=F32, glob=False):
        self.nalloc += 1
        es = self.es if glob else self.les
        return es.enter_context(self.nc.sbuf_tensor(f"{name}_{self.nalloc}", list(shape), dtype))

    def ps(self, name, shape, dtype=F32):
        self.nalloc += 1
        return self.les.enter_context(self.nc.psum_tensor(f"{name}_{self.nalloc}", list(shape), dtype))

    def _tick(self, eng):
        if eng not in self.eng_sem or self.eng_cnt[eng] >= SEM_ROLL:
            self.eng_sem[eng] = self.newsem("s" + eng)
            self.eng_cnt[eng] = 0
        self.eng_cnt[eng] += 1
        ev = (self.eng_sem[eng], self.eng_cnt[eng], eng)
        self.eng_last[eng] = ev
        return ev

    def _dma_event(self):
        if len(self.dma_sems) < N_DMA_SEMS:
            self.dma_sems.append(self.newsem("dma"))
            self.dma_cnt.append(0)
            self.dma_last.append(None)
        i = self.dma_i % N_DMA_SEMS
        self.dma_i += 1
        prev = self.dma_last[i]
        self.dma_cnt[i] += 16
        ev = (self.dma_sems[i], self.dma_cnt[i], "dma")
        self.dma_last[i] = ev
        return ev, prev

    @staticmethod
    def _deps(reads, writes):
        w = []
        for b in reads:
            if b.w is not None:
                w.append(b.w)
        for b in writes:
            if b.w is not None:
                w.append(b.w)
            w.extend(b.r.values())
        return w

    def _filter(self, eng, waits):
        wd = self.waited[eng]
        best = {}
        for ev in waits:
            sem, val, src = ev
            if eng == "pe" and src == "pe":
                continue
            k = id(sem)
            if wd.get(k, 0) >= val:
                continue
            if k not in best or best[k][1] < val:
                best[k] = ev
        out = []
        for k, ev in best.items():
            wd[k] = ev[1]
            out.append((ev[0], ev[1]))
        return out

    @staticmethod
    def _commit(ev, reads, writes):
        k = id(ev[0])
        for b in reads:
            o = b.r.get(k)
            if o is None or o[1] < ev[1]:
                b.r[k] = ev
        for b in writes:
            b.w = ev
            b.r = {}

    def op(self, eng, fn, reads=(), writes=(), extra=()):
        waits = self._filter(eng, self._deps(reads, writes) + list(extra))
        ev = self._tick(eng)
        self.q[eng].append((waits, fn, (ev[0], 1)))
        self._commit(ev, reads, writes)
        if self.yield_hook is not None:
            self.yield_hook()
        return ev

    def dma(self, eng, out, in_, reads=(), writes=(), extra=(), **kw):
        ev, prev = self._dma_event()
        deps = self._deps(reads, writes) + list(extra)
        if prev is not None:
            deps.append(prev)
        waits = self._filter(eng, deps)
        self.q[eng].append((waits, lambda e: e.dma_start(out=out, in_=in_, **kw), (ev[0], 16)))
        self._commit(ev, reads, writes)
        return ev

    def wait_all(self, eng, events):
        waits = self._filter(eng, list(events))
        if waits:
            self.q[eng].append((waits, None, None))

    def barrier(self):
        evs = [e for e in self.dma_last if e is not None] + list(self.eng_last.values())
        for e in ENGS:
            self.wait_all(e, evs)

    def flush(self, block):
        def mk(name):
            lst = self.q[name]

            def body(eng):
                for waits, fn, sig in lst:
                    for sem, val in waits:
                        eng.wait_ge(sem, val)
                    if fn is None:
                        continue
                    ins = fn(eng)
                    if sig is not None:
                        ins.then_inc(sig[0], sig[1])
            return body
        for name, starter in (("sp", block.sync), ("pe", block.tensor), ("dve", block.vector),
                              ("act", block.scalar), ("pool", block.gpsimd)):
            if self.q[name]:
                starter(mk(name))
        self.q = {e: [] for e in ENGS}


def _interleave(K, fns):
    import threading
    n = len(fns)
    sems = [threading.Semaphore(0) for _ in range(n)]
    alive = [True] * n
    done = threading.Semaphore(0)
    errs = []
    cur = [0]

    def nxt(i):
        for d in range(1, n + 1):
            j = (i + d) % n
            if alive[j]:
                return j
        return None

    def hook():
        i = cur[0]
        j = nxt(i)
        if j is None or j == i:
            return
        cur[0] = j
        sems[j].release()
        sems[i].acquire()

    def runner(i):
        sems[i].acquire()
        try:
            fns[i]()
        except BaseException as ex:
            errs.append(ex)
        alive[i] = False
        j = nxt(i)
        if j is None:
            done.release()
        else:
            cur[0] = j
            sems[j].release()

    ths = [threading.Thread(target=runner, args=(i,)) for i in range(n)]
    for t in ths:
        t.start()
    K.yield_hook = hook
    cur[0] = 0
    sems[0].release()
    done.acquire()
    K.yield_hook = None
    for t in ths:
        t.join()
    if errs:
        raise errs[0]


def _cs(i, n=128):
    return slice(i * n, (i + 1) * n)


class Prog:
    def __init__(self, nc, stop_after=None, layers=(0, 1), dbg=False):
        self.nc = nc
        self.stop_after = stop_after
        self.layers = layers
        self.dbg = dbg

    def dram_in(self, name, shape, dt=F32):
        return self.nc.dram_tensor(name, list(shape), dt, kind="ExternalInput").ap()

    def build(self):
        nc = self.nc
        self.xT = self.dram_in("xT", [D, T])
        self.cT = self.dram_in("cT", [128, 64])
        self.wada = self.dram_in("wada", [2 * 96, 128, 4096])
        self.win = self.dram_in("win", [2 * NCT_IN, 128, 4096])
        self.wout = self.dram_in("wout", [2 * 32, 128, 4096])
        self.prm = self.dram_in("prm", [128, 2, NPRM])
        self.rgw = self.dram_in("rgw", [2, 128, 2 * 8 * 128])
        self.cw1 = self.dram_in("cw1", [2, 2, 128, 32 * 256])
        self.cw2 = self.dram_in("cw2", [2, 128, 2 * 256])
        self.cf = self.dram_in("cf", [128, 352])
        self.cbt = self.dram_in("cbt", [2, 128, 6 * 512])
        self.cmask = self.dram_in("cmask", [128, 128], I32)
        self.cb = self.dram_in("cb", [128, 7168], BF16)
        okind = "ExternalOutput"
        self.outT = nc.dram_tensor("outT", [D, T], F32, kind=okind).ap()
        skind = "ExternalOutput" if self.dbg else "Internal"
        self.projT = nc.dram_tensor("projT", [NPROJ, T], F32, kind=skind).ap()
        self.yT = nc.dram_tensor("yT", [D, T], BF16, kind=skind).ap()
        self.vT = nc.dram_tensor("vT", [D, T], F32, kind="Internal").ap()
        self.x1T = nc.dram_tensor("x1T", [D, T], F32, kind=skind).ap()
        self.b_projT = Buf("projT")
        self.b_yT = Buf("yT")
        self.b_vT = Buf("vT")
        self.b_x1T = Buf("x1T")
        self.b_outT = Buf("outT")

        with ExitStack() as ges, nc.Block() as block:
            K = KB(nc, ges)
            self.K = K
            self.block = block
            self.prm_sb = K.sb("prm", [128, 2, NPRM], F32, glob=True)
            self.b_prm = Buf()
            K.dma("sp", self.prm_sb[:], self.prm[:, :, :], writes=[self.b_prm])
            self.cf_sb = K.sb("cf", [128, 352], F32, glob=True)
            self.b_cf = Buf()
            K.dma("sp", self.cf_sb[:], self.cf[:, :], writes=[self.b_cf])
            self.ident = self.cf_sb[:, 0:128]
            self.ones = self.cf_sb[:, 128:256]
            self.modT = K.sb("modT", [128, 2, 96], F32, glob=True)
            self.b_mod = Buf()
            self.misc = K.sb("misc", [128, 64], F32, glob=True)
            self.b_misc = Buf()

            self.phase0()
            done = self.stop_after == "p0"
            for l in self.layers:
                if done:
                    break
                xsrc, bxsrc = (self.xT, None) if l == 0 else (self.x1T, self.b_x1T)
                xdst, bxdst = (self.x1T, self.b_x1T) if l == 0 else (self.outT, self.b_outT)
                self.phase1(l, xsrc)
                if self.stop_after == f"p1_{l}":
                    break
                self.phase2_rghg(l)
                if self.stop_after == f"p2c_{l}":
                    break
                self.phase2_nsa(l)
                if self.stop_after == f"p2b_{l}":
                    break
                self.phase3(l, xsrc, xdst, bxdst)
                if self.stop_after == f"p3_{l}":
                    break
            K.barrier()
            K.flush(block)

    def end_phase(self):
        self.K.barrier()
        self.K.flush(self.block)

    def phase0(self):
        K = self.K
        with ExitStack() as es:
            K.les = es
            NS = 6
            wst = [K.sb("p0w", [128, 4096]) for _ in range(NS)]
            wb = [Buf() for _ in range(NS)]
            cT = K.sb("cT", [128, 64])
            bc = Buf()
            K.dma("sp", cT[:], self.cT[:, :], writes=[bc])
            cTb = K.sb("cTb", [128, 64], BF16)
            K.op("dve", (lambda e: e.tensor_copy(out=cTb[:], in_=cT[:])), reads=[bc], writes=[bc])
            wbf0 = [K.sb("p0wb", [128, 4096], BF16) for _ in range(2)]
            bwbf0 = [[Buf(), Buf()] for _ in range(2)]
            psr = [K.ps("psrow", [128, 512]) for _ in range(2)]
            bpsr = [Buf() for _ in range(2)]
            pst = K.ps("pst", [128, 512])
            bpst = Buf()
            ROW = K.sb("ROW", [2, 12288])
            bROW = Buf()
            for l in range(2):
                for cbk in range(24):
                    pb = cbk % 2
                    for q in range(4):
                        g = l * 96 + cbk * 4 + q
                        i = g % NS
                        K.dma("sp", wst[i][:], self.wada[g, :, :], writes=[wb[i]])
                        i2 = g % 2
                        for hf in range(2):
                            if hf == 0:
                                K.op("dve", (lambda e, i=i, i2=i2, hf=hf: e.tensor_copy(out=wbf0[i2][:, hf * 2048:(hf + 1) * 2048], in_=wst[i][:, hf * 2048:(hf + 1) * 2048])),
                                     reads=[wb[i]], writes=[bwbf0[i2][hf]])
                            else:
                                K.op("act", (lambda e, i=i, i2=i2, hf=hf: e.copy(out=wbf0[i2][:, hf * 2048:(hf + 1) * 2048], in_=wst[i][:, hf * 2048:(hf + 1) * 2048])),
                                     reads=[wb[i]], writes=[bwbf0[i2][hf]])
                        for k8 in range(8):
                            kc = q * 8 + k8
                            K.op("pe", (lambda e, i2=i2, kc=kc, k8=k8, pb=pb: e.matmul(
                                psr[pb][0:2, :], lhsT=cTb[:, 2 * kc:2 * kc + 2], rhs=wbf0[i2][:, k8 * 512:(k8 + 1) * 512],
                                start=(kc == 0), stop=(kc == NKC - 1))), reads=[bwbf0[i2][k8 // 4], bc], writes=[bpsr[pb]])
                    K.op("act", (lambda e, pb=pb, cbk=cbk: e.copy(out=ROW[0:2, cbk * 512:(cbk + 1) * 512], in_=psr[pb][0:2, :])),
                         reads=[bpsr[pb]], writes=[bROW])
                for nt in range(96):
                    K.op("pe", (lambda e, nt=nt: e.transpose(out=pst[:, 2 * nt:2 * nt + 2], in_=ROW[0:2, _cs(nt)], identity=self.ident[0:2, 0:2])),
                         reads=[bROW, self.b_cf], writes=[bpst])
                src = pst[:, 0:192].rearrange("p (n two) -> p n two", two=2)[:, :, 0]
                K.op("dve", (lambda e, l=l, src=src: e.tensor_tensor(
                    out=self.modT[:, l, :], in0=src, in1=self.prm_sb[:, l, P_BADA:P_BADA + 96], op=ALU.add)),
                    reads=[bpst, self.b_prm], writes=[self.b_mod])
                K.op("dve", (lambda e, l=l: e.tensor_scalar_add(
                    out=self.modT[:, l, 32:96], in0=self.modT[:, l, 32:96], scalar1=1.0)),
                    reads=[self.b_mod], writes=[self.b_mod])
            m = self.misc
            for l in range(2):
                K.op("act", (lambda e, l=l: e.activation(out=m[:, 48 + 0:48 + 8], in_=self.prm_sb[:, l, P_LAM:P_LAM + 8],
                                                          func=AF.Exp, scale=-1.0)), reads=[self.b_prm], writes=[self.b_misc])
                K.op("act", (lambda e: e.activation(out=m[:, 48:56], in_=m[:, 48:56], func=AF.Ln, bias=1.0)),
                     reads=[self.b_misc], writes=[self.b_misc])
                K.op("dve", (lambda e, l=l: e.tensor_scalar_mul(out=m[:, l * 16:l * 16 + 8], in0=m[:, 48:56], scalar1=-8.0)),
                     reads=[self.b_misc], writes=[self.b_misc])
                K.op("dve", (lambda e, l=l: e.tensor_scalar_mul(out=m[:, l * 16 + 8:l * 16 + 16], in0=m[:, 48:56], scalar1=-16.0)),
                     reads=[self.b_misc], writes=[self.b_misc])
            K.op("dve", (lambda e: e.memset(m[:, 32:40], 1.0)), writes=[self.b_misc])
            K.op("dve", (lambda e: e.tensor_tensor(out=m[:, 56:64], in0=self.prm_sb[:, 0, P_LB:P_LB + 8],
                                                   in1=self.prm_sb[:, 0, P_LB + 8:P_LB + 16], op=ALU.subtract)),
                 reads=[self.b_prm], writes=[self.b_misc])
            K.op("act", (lambda e: e.activation(out=m[:, 40:48], in_=m[:, 56:64], func=AF.Sigmoid)),
                 reads=[self.b_misc], writes=[self.b_misc])
            self.end_phase()

    def _proj_loop(self, big, bbig, wsrc, wbase, nct, consume, n_wst=3):
        K = self.K
        wst = [K.sb("wst", [128, 2048]) for _ in range(n_wst)]
        bwst = [Buf() for _ in range(n_wst)]
        wbf = [K.sb("wbf", [128, NKC, 128], BF16) for _ in range(2)]
        bwbf = [Buf() for _ in range(2)]
        ps = [K.ps("pj", [128, 512]) for _ in range(8)]
        bps = [Buf() for _ in range(8)]
        hi = [0]

        def load(ct):
            s = ct % 2
            for half in range(2):
                j = hi[0] % n_wst
                hi[0] += 1
                K.dma("sp", wst[j][:], wsrc[wbase + ct, :, half * 2048:(half + 1) * 2048], writes=[bwst[j]])
                K.op("dve", (lambda e, j=j, s=s, half=half: e.tensor_copy(
                    out=wbf[s][:, half * 16:(half + 1) * 16, :],
                    in_=wst[j][:, :].rearrange("p (k c) -> p k c", c=128))),
                    reads=[bwst[j]], writes=[bwbf[s]])

        load(0)
        for ct in range(nct):
            s = ct % 2
            if ct + 1 < nct:
                load(ct + 1)
            pp = (ct % 2) * 4
            for kc in range(NKC):
                for tb in range(4):
                    K.op("pe", (lambda e, s=s, kc=kc, tb=tb, pp=pp: e.matmul(
                        ps[pp + tb][:, :], lhsT=wbf[s][:, kc, :], rhs=big[:, kc, tb * 512:(tb + 1) * 512],
                        start=(kc == 0), stop=(kc == NKC - 1))),
                        reads=[bwbf[s], bbig], writes=[bps[pp + tb]])
            for tb in range(4):
                consume(ct, tb, ps[pp + tb], bps[pp + tb])

    def phase1(self, l, xsrc):
        K = self.K
        with ExitStack() as es:
            K.les = es
            big = K.sb("big", [128, NKC, T], BF16)
            bbig = Buf()
            xst = [K.sb("xst", [128, 1024]) for _ in range(2)]
            bxst = [Buf() for _ in range(2)]
            n = 0
            for kc in range(NKC):
                for hf in range(2):
                    i = n % 2
                    n += 1
                    K.dma("act" if n % 2 else "sp", xst[i][:], xsrc[_cs(kc), hf * 1024:(hf + 1) * 1024], writes=[bxst[i]])
                    if hf == 0:
                        K.op("act", (lambda e, i=i, kc=kc, hf=hf: e.activation(
                            out=big[:, kc, hf * 1024:(hf + 1) * 1024], in_=xst[i][:], func=AF.Identity,
                            scale=self.modT[:, l, 32 + kc:33 + kc], bias=self.modT[:, l, kc:kc + 1])),
                            reads=[bxst[i], self.b_mod], writes=[bbig])
                    else:
                        K.op("dve", (lambda e, i=i, kc=kc, hf=hf: e.tensor_scalar(
                            out=big[:, kc, hf * 1024:(hf + 1) * 1024], in0=xst[i][:], scalar1=self.modT[:, l, 32 + kc:33 + kc],
                            scalar2=self.modT[:, l, kc:kc + 1], op0=ALU.mult, op1=ALU.add)),
                            reads=[bxst[i], self.b_mod], writes=[bbig])
            ev = [K.sb("ev", [128, 512]) for _ in range(4)]
            bev = [Buf() for _ in range(4)]
            cnt = [0]

            def consume(ct, tb, ps, bps):
                k = cnt[0] % 4
                cnt[0] += 1
                if (8 <= ct < 16) or (44 <= ct < 68) or (84 <= ct < 92):
                    K.op("act", (lambda e, k=k, ps=ps: e.activation(out=ev[k][:], in_=ps[:, :], func=AF.Silu)), reads=[bps], writes=[bev[k]])
                elif ct == 92:
                    K.op("act", (lambda e, k=k, ps=ps: e.activation(out=ev[k][:], in_=ps[:, :], func=AF.Sigmoid)), reads=[bps], writes=[bev[k]])
                else:
                    K.op("act", (lambda e, k=k, ps=ps: e.copy(out=ev[k][:], in_=ps[:, :])), reads=[bps], writes=[bev[k]])
                K.dma("act", self.projT[_cs(ct), tb * 512:(tb + 1) * 512], ev[k][:], reads=[bev[k]])

            self._proj_loop(big, bbig, self.win, l * NCT_IN, NCT_IN, consume)
            self.end_phase()

    def phase2_rghg(self, l):
        K = self.K
        with ExitStack() as es:
            K.les = es
            _interleave(K, [lambda: self.phase2_rg(l, (0, 2, 4, 6)), lambda: self.phase2_rg(l, (1, 3, 5, 7))])
            self.end_phase()
        with ExitStack() as es:
            K.les = es
            RM = K.sb("hgRM", [128, T])
            bRM = Buf()
            K.op("pool", lambda e: e.memset(RM[:], 1.0), writes=[bRM])
            K.op("pool", lambda e: e.memset(RM[:, :].rearrange("p (c t) -> p c t", t=128)[:, :, 0:1], 0.0), writes=[bRM])
            _interleave(K, [lambda: self.phase2_hg(l, range(0, 4), RM, bRM), lambda: self.phase2_hg(l, range(4, 8), RM, bRM)])
            self.end_phase()

    def phase2_rg(self, l, ns):
        K = self.K
        P = self.prm_sb
        if True:
            wg = K.sb("rgw", [128, 2, 8, 128])
            bwg = Buf()
            K.dma("sp", wg[:].rearrange("p a n e -> p (a n e)"), self.rgw[l, :, :], writes=[bwg])
            xp = K.sb("xp", [128, T + 3]); bxp = Buf()
            xc = K.sb("xc", [128, T]); bxc = Buf()
            R = K.sb("R", [128, T]); bR = Buf()
            I_ = K.sb("I", [128, T]); bI = Buf()
            A = K.sb("A", [128, T]); bA = Buf()
            A2 = K.sb("A2", [128, T]); bA2 = Buf()
            H = K.sb("H", [128, T]); bH = Buf()
            G = K.sb("G", [128, T]); bG = Buf()
            Y = K.sb("Y", [128, T], BF16); bY = Buf()
            ps = [K.ps("rgp", [128, 512]) for _ in range(2)]
            bps = [Buf() for _ in range(2)]
            K.op("pool", lambda e: e.memset(xp[:, 0:3], 0.0), writes=[bxp])
            c8 = self.misc[:, l * 16:l * 16 + 8]
            c16 = self.misc[:, l * 16 + 8:l * 16 + 16]
            for n in ns:
                K.dma("sp", xp[:, 3:], self.projT[OFF["rgx"] + n * 128:OFF["rgx"] + (n + 1) * 128, :], writes=[bxp])
                K.dma("act", G[:], self.projT[OFF["rgg"] + n * 128:OFF["rgg"] + (n + 1) * 128, :], writes=[bG])
                K.op("dve", (lambda e, n=n: e.tensor_scalar(
                    out=xc[:], in0=xp[:, 0:T], scalar1=P[:, l, P_CONVW + n * 4:P_CONVW + n * 4 + 1],
                    scalar2=P[:, l, P_CONVB + n:P_CONVB + n + 1], op0=ALU.mult, op1=ALU.add)),
                    reads=[bxp, self.b_prm], writes=[bxc])
                for j in range(1, 4):
                    K.op("dve", (lambda e, n=n, j=j: e.scalar_tensor_tensor(
                        out=xc[:], in0=xp[:, j:j + T], scalar=P[:, l, P_CONVW + n * 4 + j:P_CONVW + n * 4 + j + 1],
                        in1=xc[:], op0=ALU.mult, op1=ALU.add)), reads=[bxp, bxc], writes=[bxc])
                pi = 0
                for gi, (dst, bdst, pb) in enumerate(((R, bR, P_BA), (I_, bI, P_BX))):
                    for tb in range(4):
                        p = pi % 2
                        pi += 1
                        K.op("pe", (lambda e, gi=gi, n=n, tb=tb, p=p: e.matmul(
                            ps[p][:, :], lhsT=wg[:, gi, n, :], rhs=xc[:, tb * 512:(tb + 1) * 512], start=True, stop=True)),
                            reads=[bwg, bxc], writes=[bps[p]])
                        K.op("act", (lambda e, dst=dst, tb=tb, p=p, pb=pb, n=n: e.activation(
                            out=dst[:, tb * 512:(tb + 1) * 512], in_=ps[p][:, :], func=AF.Sigmoid,
                            bias=P[:, l, pb + n:pb + n + 1])), reads=[bps[p], self.b_prm], writes=[bdst])
                K.op("act", (lambda e, n=n: e.activation(out=A[:], in_=R[:], func=AF.Exp, scale=c8[:, n:n + 1])),
                     reads=[bR, self.b_misc], writes=[bA])
                K.op("act", (lambda e, n=n: e.activation(out=A2[:], in_=R[:], func=AF.Exp, scale=c16[:, n:n + 1])),
                     reads=[bR, self.b_misc], writes=[bA2])
                K.op("dve", (lambda e: e.tensor_scalar_min(out=A2[:], in0=A2[:], scalar1=1.0 - 6e-8)), reads=[bA2], writes=[bA2])
                K.op("act", (lambda e: e.activation(out=A2[:], in_=A2[:], func=AF.Sqrt, scale=-1.0, bias=1.0)),
                     reads=[bA2], writes=[bA2])
                K.op("dve", (lambda e: e.memset(A2[:, 0:1], 1.0)), writes=[bA2])
                K.op("dve", (lambda e: e.tensor_tensor(out=I_[:], in0=I_[:], in1=xc[:], op=ALU.mult)), reads=[bI, bxc], writes=[bI])
                K.op("dve", (lambda e: e.tensor_tensor(out=I_[:], in0=I_[:], in1=A2[:], op=ALU.mult)), reads=[bI, bA2], writes=[bI])
                K.op("dve", (lambda e: e.tensor_tensor_scan(out=H[:], data0=A[:], data1=I_[:], initial=0.0,
                                                            op0=ALU.mult, op1=ALU.add)), reads=[bA, bI], writes=[bH])
                K.op("dve", (lambda e: e.tensor_tensor(out=Y[:], in0=H[:], in1=G[:], op=ALU.mult)), reads=[bH, bG], writes=[bY])
                K.dma("sp", self.yT[_cs(n), :], Y[:], reads=[bY])

    def phase2_hg(self, l, heads, RM, bRM):
        K = self.K
        P = self.prm_sb
        if True:
            names = ["Z", "Q", "G", "Kb", "B", "E", "QT", "KT", "O"]
            tl = {n: K.sb("hg" + n, [128, T]) for n in names}
            bf = {n: Buf(n) for n in names}
            Z, Q, G, Kb, B, E, QT, KT, O = (tl[n] for n in names)
            tl["VT"] = E
            bf["VT"] = bf["E"]
            VT = E
            bf["RM"] = bRM
            VTM = K.sb("VTM", [128, 16, 128]); bVTM = Buf()
            Yb = K.sb("hgY", [128, T], BF16); bYb = Buf()
            cm = K.sb("cm", [128, 128], I32); bcm = Buf()
            K.dma("sp", cm[:], self.cmask[:, :], writes=[bcm])
            S = [K.sb("S", [128, 128]) for _ in range(2)]; bS = [Buf() for _ in range(2)]
            ATs = [K.sb("ATs", [128, 128]) for _ in range(2)]; bATs = [Buf() for _ in range(2)]
            KDT = [K.sb("KDT", [128, 128]) for _ in range(2)]; bKDT = [Buf() for _ in range(2)]
            EBL = K.sb("EBL", [128, 16]); bEBL = Buf()
            _pa = K.ps("hpa", [128, 512]); _bpa = Buf()
            ps_a = [_pa, _pa]; bpa = [_bpa, _bpa]
            _po = K.ps("hpo", [128, 512]); _bpo = Buf()
            ps_o = [_po, _po]; bpo = [_bpo, _bpo]
            _pts = K.ps("hpts", [128, 512])
            ps_t = _pts[:, 0:256]; bpt = Buf()
            ps_s = _pts[:, 256:512]; bpss = Buf()
            _pm = K.ps("hpm", [128, 512]); _bpm = Buf()
            ps_m = [_pm, _pm]; bpm = [_bpm, _bpm]
            oml = self.misc[:, 32 + l * 8:32 + l * 8 + 8]
            B3 = B[:, :].rearrange("p (c t) -> p c t", t=128)
            E3 = E[:, :].rearrange("p (c t) -> p c t", t=128)
            for h in heads:
                for nm, off, q in (("Z", "hf", "sp"), ("Q", "hq", "act"), ("VT", "hi", "sp"), ("G", "hg", "act")):
                    K.dma(q, tl[nm][:], self.projT[OFF[off] + h * 128:OFF[off] + (h + 1) * 128, :], writes=[bf[nm]])
                for c4 in range(4):
                    p = c4 % 2
                    for j in range(4):
                        c = c4 * 4 + j
                        K.op("pe", (lambda e, p=p, j=j, c=c: e.transpose(out=ps_m[p][:, _cs(j)], in_=VT[:, _cs(c)], identity=self.ident)),
                             reads=[bf["VT"], self.b_cf], writes=[bpm[p]])
                    K.op("act", (lambda e, p=p, c4=c4: e.copy(out=VTM[:, c4 * 4:(c4 + 1) * 4, :].rearrange("p c d -> p (c d)"), in_=ps_m[p][:, :])),
                         reads=[bpm[p]], writes=[bVTM])
                K.op("act", (lambda e: e.activation(out=Kb[:], in_=Z[:], func=AF.Sigmoid, scale=-1.0)), reads=[bf["Z"]], writes=[bf["Kb"]])
                K.op("dve", (lambda e, h=h: e.tensor_scalar(out=Kb[:], in0=Kb[:], scalar1=oml[:, h:h + 1], scalar2=None, op0=ALU.mult)),
                     reads=[bf["Kb"], self.b_misc], writes=[bf["Kb"]])
                K.op("dve", (lambda e: e.tensor_scalar(out=Z[:], in0=Kb[:], scalar1=-1.0, scalar2=1.0, op0=ALU.mult, op1=ALU.add)),
                     reads=[bf["Kb"]], writes=[bf["Z"]])
                K.op("act", (lambda e: e.activation(out=Z[:], in_=Z[:], func=AF.Ln)), reads=[bf["Z"]], writes=[bf["Z"]])
                K.op("dve", (lambda e: e.tensor_tensor_scan(out=B[:], data0=RM[:], data1=Z[:], initial=0.0, op0=ALU.mult, op1=ALU.add)),
                     reads=[bf["RM"], bf["Z"]], writes=[bf["B"]])
                K.op("dve", (lambda e: e.tensor_tensor(out=E3, in0=B3, in1=B3[:, :, 63:64].to_broadcast([128, 16, 128]), op=ALU.subtract)),
                     reads=[bf["B"]], writes=[bf["E"]])
                K.op("act", (lambda e: e.activation(out=QT[:], in_=E[:], func=AF.Exp)), reads=[bf["E"]], writes=[bf["QT"]])
                K.op("dve", (lambda e: e.tensor_tensor(out=QT[:], in0=QT[:], in1=Q[:], op=ALU.mult)), reads=[bf["QT"], bf["Q"]], writes=[bf["QT"]])
                K.op("act", (lambda e: e.activation(out=KT[:], in_=E[:], func=AF.Exp, scale=-1.0)), reads=[bf["E"]], writes=[bf["KT"]])
                K.op("dve", (lambda e: e.tensor_tensor(out=KT[:], in0=KT[:], in1=Kb[:], op=ALU.mult)), reads=[bf["KT"], bf["Kb"]], writes=[bf["KT"]])
                K.op("act", (lambda e: e.activation(out=E[:], in_=B[:], func=AF.Exp)), reads=[bf["B"]], writes=[bf["E"]])
                K.op("dve", (lambda e: e.tensor_tensor(out=Q[:], in0=Q[:], in1=E[:], op=ALU.mult)), reads=[bf["Q"], bf["E"]], writes=[bf["Q"]])
                K.op("act", (lambda e: e.activation(out=EBL[:], in_=B3[:, :, 127], func=AF.Exp)), reads=[bf["B"]], writes=[bEBL])
                K.op("dve", (lambda e: e.tensor_tensor(out=E3, in0=B3, in1=B3[:, :, 127:128].to_broadcast([128, 16, 128]), op=ALU.subtract)),
                     reads=[bf["B"]], writes=[bf["E"]])
                K.op("act", (lambda e: e.activation(out=E[:], in_=E[:], func=AF.Exp, scale=-1.0)), reads=[bf["E"]], writes=[bf["E"]])
                K.op("dve", (lambda e: e.tensor_tensor(out=Kb[:], in0=Kb[:], in1=E[:], op=ALU.mult)), reads=[bf["Kb"], bf["E"]], writes=[bf["Kb"]])
                for c in range(16):
                    a = c % 2
                    K.op("pe", (lambda e, a=a, c=c: e.matmul(ps_a[a][:, 0:128], lhsT=KT[:, _cs(c)], rhs=QT[:, _cs(c)], start=True, stop=True)),
                         reads=[bf["KT"], bf["QT"]], writes=[bpa[a]])
                    K.op("pool", (lambda e, a=a: e.memset(ATs[a][:], 0.0)), writes=[bATs[a]])
                    K.op("dve", (lambda e, a=a: e.copy_predicated(out=ATs[a][:], mask=cm[:], data=ps_a[a][:, 0:128])),
                         reads=[bpa[a], bcm], writes=[bATs[a]])
                    K.op("pe", (lambda e, a=a, c=c: e.matmul(ps_o[a][:, 0:128], lhsT=VTM[:, c, :], rhs=ATs[a][:], start=True, stop=(c == 0))),
                         reads=[bVTM, bATs[a]], writes=[bpo[a]])
                    if c > 0:
                        K.op("pe", (lambda e, a=a, c=c: e.matmul(ps_o[a][:, 0:128], lhsT=S[(c - 1) % 2][:], rhs=Q[:, _cs(c)], start=False, stop=True)),
                             reads=[bS[(c - 1) % 2], bf["Q"]], writes=[bpo[a]])
                    K.op("act", (lambda e, a=a, c=c: e.copy(out=O[:, _cs(c)], in_=ps_o[a][:, 0:128])), reads=[bpo[a]], writes=[bf["O"]])
                    if c < 15:
                        K.op("pe", (lambda e, c=c: e.transpose(out=ps_t[:, 0:128], in_=Kb[:, _cs(c)], identity=self.ident)),
                             reads=[bf["Kb"], self.b_cf], writes=[bpt])
                        K.op("act", (lambda e, a=a: e.copy(out=KDT[a][:], in_=ps_t[:, 0:128])), reads=[bpt], writes=[bKDT[a]])
                        K.op("pe", (lambda e, a=a, c=c: e.matmul(ps_s[:, 0:128], lhsT=KDT[a][:], rhs=VTM[:, c, :], start=True, stop=True)),
                             reads=[bKDT[a], bVTM], writes=[bpss])
                        if c == 0:
                            K.op("dve", (lambda e: e.tensor_copy(out=S[0][:], in_=ps_s[:, 0:128])), reads=[bpss], writes=[bS[0]])
                        else:
                            K.op("dve", (lambda e, c=c: e.scalar_tensor_tensor(
                                out=S[c % 2][:], in0=S[(c - 1) % 2][:], scalar=EBL[:, c:c + 1], in1=ps_s[:, 0:128],
                                op0=ALU.mult, op1=ALU.add)), reads=[bS[(c - 1) % 2], bEBL, bpss], writes=[bS[c % 2]])
                K.op("act", (lambda e: e.activation(out=E[:], in_=O[:], func=AF.Square)), reads=[bf["O"]], writes=[bf["E"]])
                for tb in range(4):
                    p = tb % 2
                    K.op("pe", (lambda e, p=p, tb=tb: e.matmul(ps_m[p][:, :], lhsT=self.ones, rhs=E[:, tb * 512:(tb + 1) * 512], start=True, stop=True)),
                         reads=[self.b_cf, bf["E"]], writes=[bpm[p]])
                    K.op("dve", (lambda e, p=p, tb=tb: e.tensor_scalar(out=QT[:, tb * 512:(tb + 1) * 512], in0=ps_m[p][:, :], scalar1=1.0 / 128,
                                                                   scalar2=RMS_EPS, op0=ALU.mult, op1=ALU.add)), reads=[bpm[p]], writes=[bf["QT"]])
                K.op("act", (lambda e: e.activation(out=QT[:], in_=QT[:], func=AF.Sqrt)), reads=[bf["QT"]], writes=[bf["QT"]])
                K.op("dve", (lambda e: e.reciprocal(out=QT[:], in_=QT[:])), reads=[bf["QT"]], writes=[bf["QT"]])
                K.op("dve", (lambda e: e.tensor_tensor(out=O[:], in0=O[:], in1=QT[:], op=ALU.mult)), reads=[bf["O"], bf["QT"]], writes=[bf["O"]])
                K.op("dve", (lambda e: e.scalar_tensor_tensor(out=Yb[:], in0=O[:], scalar=P[:, l, P_NG:P_NG + 1], in1=G[:],
                                                              op0=ALU.mult, op1=ALU.mult)), reads=[bf["O"], bf["G"], self.b_prm], writes=[bYb])
                K.dma("sp", self.yT[_cs(24 + h), :], Yb[:], reads=[bYb])

    def phase2_nsa(self, l):
        K = self.K
        P = self.prm_sb
        with ExitStack() as es:
            K.les = es
            cb = K.sb("cb", [128, 7168], BF16); bcb = Buf()
            K.dma("sp", cb[:], self.cb[:, :], writes=[bcb])
            LD = cb[:, 0:4096].rearrange("p (d j) -> p d j", j=128)
            EX = cb[:, 5120:7168].rearrange("p (k j) -> p k j", j=128)
            BT = K.sb("BT", [128, 6, 512]); bBT = Buf()
            qTb = K.sb("qTb", [128, 8, T], BF16); bq = Buf()
            kTs = K.sb("kTs", [128, T], BF16); bkTs = Buf()
            kTw = K.sb("kTw", [128, T], BF16); bkTw = Buf()
            V1s = K.sb("V1s", [128, 16, 129], BF16); bV1s = Buf()
            V1w = K.sb("V1w", [128, 16, 129], BF16); bV1w = Buf()
            KC = K.sb("KC", [128, T]); bKC = Buf()
            stg = [K.sb("stg", [128, T]) for _ in range(2)]; bstg = [Buf() for _ in range(2)]
            W1 = K.sb("W1", [128, 32, 128]); bW1 = Buf()
            W2 = K.sb("W2", [128, 2, 2, 128]); bW2 = Buf()
            K.dma("sp", W2[:].rearrange("p a h d -> p (a h d)"), self.cw2[l, :, :], writes=[bW2])
            HS = K.sb("HS", [128, 2, 128]); bHS = Buf()
            CST = K.sb("CST", [128, 2]); bCST = Buf()
            KCMP = K.sb("KCMP", [128, 128], BF16); bKCMP = Buf()
            VC1 = K.sb("VC1", [128, 161]); bVC1 = Buf()
            GLT = K.sb("GLT", [128, 32, 128]); bGLT = Buf()
            K.op("pool", lambda e: e.memset(GLT[:], 0.0), writes=[bGLT])
            SG = [K.sb("SG", [128, 4, 3]) for _ in range(2)]; bSG = [Buf() for _ in range(2)]
            FOLD = K.sb("FOLD", [128, 128]); bFOLD = Buf()
            K.op("pool", lambda e: e.memset(FOLD[:], 0.0), writes=[bFOLD])
            K.dma("sp", FOLD[:, 0:64], self.cf[:, 288:352], writes=[bFOLD])
            IMP2 = K.sb("IMP2", [128, 32]); bIMP2 = Buf()
            SGATE = [K.sb("SGATE", [128, 8, 256]) for _ in range(2)]; bSGATE = [Buf() for _ in range(2)]
            s_sb = [K.sb("s_sb", [128, 512]) for _ in range(3)]; bs_sb = [Buf() for _ in range(3)]
            s_sb2 = K.sb("s_sb2", [128, 512]); bs_sb2 = Buf()
            ecf = K.sb("ecf", [128, 512]); becf = Buf()
            eb = [K.sb("eb", [128, 512], BF16) for _ in range(3)]; beb = [Buf() for _ in range(3)]
            ACCs = [K.sb("ACC", [128, 4, 128]) for _ in range(2)]; bACCs = [Buf() for _ in range(2)]
            RD = K.sb("RD", [128, 4]); bRD = Buf()
            WG = K.sb("WG", [128, 4]); bWG = Buf()
            UT = K.sb("UT", [128, 4, 32]); bUT = Buf()
            IMP = K.sb("IMP", [64, 32]); bIMP = Buf()
            TMP = K.sb("TMP", [64, 32]); bTMP = Buf()
            M8 = K.sb("M8", [64, 16]); bM8 = Buf()
            MK = K.sb("MK", [128, 128]); bMK = Buf()
            K.op("pool", lambda e: e.memset(MK[:], 0.0), writes=[bMK])
            MBT = K.sb("MBT", [128, 8, 64], BF16); bMBT = Buf()
            K.op("pool", lambda e: e.memset(MBT[:], 0.0), writes=[bMBT])
            Yt = [K.sb("Yt", [128, 8, 64], BF16) for _ in range(2)]; bYt = [Buf() for _ in range(2)]
            OS = [K.sb("OS", [128, 2, 512]) for _ in range(2)]; bOS = [Buf() for _ in range(2)]
            osi = [0]
            ps_s = [K.ps("nps", [128, 512]) for _ in range(3)]; bps_s = [Buf() for _ in range(3)]
            psO = [K.ps("npo", [128, 512]) for _ in range(2)]; bpsO = Buf()
            ps_y = K.ps("npy", [128, 512]); bps_y = Buf()
            _pm = K.ps("npm", [128, 512]); _bpm = Buf()
            ps_m = [_pm, _pm]; bps_m = [_bpm, _bpm]
            K.op("pool", lambda e: e.memset(V1s[:, :, 128:129], 1.0), writes=[bV1s])
            K.op("pool", lambda e: e.memset(V1w[:, :, 128:129], 1.0), writes=[bV1w])
            K.op("pool", lambda e: e.memset(KCMP[:], 0.0), writes=[bKCMP])
            K.op("pool", lambda e: e.memset(VC1[:], 0.0), writes=[bVC1])
            K.op("pool", lambda e: e.memset(VC1[:, 128:129], 1.0), writes=[bVC1])
            K.dma("sp", VC1[0:127, 129:161], self.cf[0:127, 256:288], writes=[bVC1])
            sti = [0]
            freg = []

            def fillreg(e):
                if not freg:
                    freg.append(e.to_reg(NEG))
                return freg[0]

            def load_stage(row0):
                i = sti[0] % 2
                sti[0] += 1
                K.dma("sp" if i else "act", stg[i][:], self.projT[row0:row0 + 128, :], writes=[bstg[i]])
                return stg[i], bstg[i]

            def opsO(r):
                return psO[r // 3][0:64, (r % 3) * 161:(r % 3) * 161 + 161]

            for g in range(2):
                SC = cb[:, 4096 + g * 512:4096 + (g + 1) * 512]
                K.dma("sp", BT[:].rearrange("p a n -> p (a n)"), self.cbt[g, :, :], writes=[bBT])
                glsrc = self.projT[OFF["gl"] + g * 24:OFF["gl"] + (g + 1) * 24, :].rearrange("p (q t) -> p q t", t=64)
                K.dma("sp", GLT[0:24, :, 0:64], glsrc, writes=[bGLT])
                K.dma("act", GLT[0:24, :, 64:128], glsrc, writes=[bGLT])
                for r in range(8):
                    st, bst = load_stage(OFF["q"] + (g * 8 + r) * 128)
                    K.op("dve" if r % 2 else "act", (lambda e, st=st, r=r: (e.tensor_copy(out=qTb[:, r, :], in_=st[:]) if r % 2 else e.copy(out=qTb[:, r, :], in_=st[:]))),
                         reads=[bst], writes=[bq])
                for dst, bdst, off in ((kTs, bkTs, OFF["kvs"]), (kTw, bkTw, OFF["kvw"])):
                    st, bst = load_stage(off + g * 128)
                    K.op("dve", (lambda e, st=st, dst=dst: e.tensor_copy(out=dst[:], in_=st[:])), reads=[bst], writes=[bdst])
                for dst, bdst, off in ((V1s, bV1s, OFF["kvs"]), (V1w, bV1w, OFF["kvw"])):
                    st, bst = load_stage(off + 256 + g * 128)
                    for c4 in range(4):
                        p = c4 % 2
                        for j in range(4):
                            c = c4 * 4 + j
                            K.op("pe", (lambda e, p=p, j=j, c=c, st=st: e.transpose(out=ps_m[p][:, _cs(j)], in_=st[:, _cs(c)], identity=self.ident)),
                                 reads=[bst, self.b_cf], writes=[bps_m[p]])
                        K.op("act", (lambda e, p=p, c4=c4, dst=dst: e.copy(out=dst[:, c4 * 4:(c4 + 1) * 4, 0:128],
                                                                       in_=ps_m[p][:, :].rearrange("p (c d) -> p c d", d=128))),
                             reads=[bps_m[p]], writes=[bdst])
                for xi, (off, pcol) in enumerate(((OFF["kvc"] + g * 128, P_PEK), (OFF["kvc"] + 256 + g * 128, P_PEV))):
                    K.dma("sp", KC[:], self.projT[off:off + 128, :], writes=[bKC])
                    for hc in range(2):
                        K.dma("act", W1[:], self.cw1[l, xi, :, :].rearrange("p (l h) -> p l h", h=256)[:, :, _cs(hc)], writes=[bW1])
                        for li in range(32):
                            K.op("pe", (lambda e, hc=hc, li=li, pcol=pcol: e.matmul(
                                ps_m[0][:, 256 + 2 * hc:256 + 2 * hc + 1], lhsT=W1[:, li, :], rhs=P[:, l, pcol + li:pcol + li + 1],
                                start=(li == 0), stop=(li == 31))), reads=[bW1, self.b_prm], writes=[bps_m[0]])
                        K.op("dve", (lambda e, hc=hc: e.tensor_copy(out=CST[:, hc:hc + 1], in_=ps_m[0][:, 256 + 2 * hc:256 + 2 * hc + 1])),
                             reads=[bps_m[0]], writes=[bCST])
                        for li in range(32):
                            K.op("pe", (lambda e, hc=hc, li=li: e.matmul(
                                ps_m[1][:, hc * 128:hc * 128 + 127], lhsT=W1[:, li, :], rhs=KC[:, li:li + 2017:16],
                                start=(li == 0), stop=(li == 31))), reads=[bW1, bKC], writes=[bps_m[1]])
                        K.op("act", (lambda e, hc=hc: e.activation(out=HS[:, hc, 0:127], in_=ps_m[1][:, hc * 128:hc * 128 + 127],
                                                                   func=AF.Silu, bias=CST[:, hc:hc + 1])), reads=[bps_m[1], bCST], writes=[bHS])
                    if xi == 0:
                        for hc in range(2):
                            K.op("pe", (lambda e, hc=hc: e.matmul(ps_m[0][:, 0:127], lhsT=W2[:, 0, hc, :], rhs=HS[:, hc, 0:127],
                                                                  start=(hc == 0), stop=(hc == 1))), reads=[bW2, bHS], writes=[bps_m[0]])
                        K.op("act", (lambda e: e.copy(out=KCMP[:, 0:127], in_=ps_m[0][:, 0:127])), reads=[bps_m[0]], writes=[bKCMP])
                    else:
                        for hc in range(2):
                            K.op("pe", (lambda e, hc=hc: e.matmul(ps_m[0][0:127, 0:128], lhsT=HS[:, hc, 0:127], rhs=W2[:, 1, hc, :],
                                                                  start=(hc == 0), stop=(hc == 1))), reads=[bW2, bHS], writes=[bps_m[0]])
                        K.op("act", (lambda e: e.copy(out=VC1[0:127, 0:128], in_=ps_m[0][0:127, 0:128])), reads=[bps_m[0]], writes=[bVC1])
                pair = [0]

                def score_pair(qb, lhsT_main, breads, delta, bias_idx, mask_kt, nrows=128, SC=SC):
                    i = pair[0] % 3
                    j = pair[0] % 3
                    pair[0] += 1
                    rq = qTb[:, :, qb * 64:(qb + 1) * 64]
                    last = "main" if (delta == 0 and mask_kt is None) else ("mask" if mask_kt is not None else "delta")
                    K.op("pe", (lambda e, i=i, rq=rq: e.matmul(ps_s[i][0:nrows, :], lhsT=lhsT_main, rhs=rq, start=True, stop=(last == "main"))),
                         reads=list(breads) + [bq], writes=[bps_s[i]])
                    if delta > 0:
                        K.op("pe", (lambda e, i=i: e.matmul(ps_s[i][0:nrows, :], lhsT=LD[:, delta, 0:nrows], rhs=SC, start=False, stop=(last == "delta"))),
                             reads=[bcb], writes=[bps_s[i]])
                    if mask_kt is not None:
                        K.op("pe", (lambda e, i=i: e.matmul(ps_s[i][0:nrows, :], lhsT=EX[:, mask_kt, :], rhs=MBT[:].rearrange("p r t -> p (r t)"),
                                                            start=False, stop=True)), reads=[bcb, bMBT], writes=[bps_s[i]])
                    K.op("dve", (lambda e, i=i, j=j: e.scalar_tensor_tensor(out=s_sb[j][0:nrows, :], in0=ps_s[i][0:nrows, :], scalar=SCALE,
                                                                            in1=BT[0:nrows, bias_idx, :], op0=ALU.mult, op1=ALU.add)),
                         reads=[bps_s[i], bBT], writes=[bs_sb[j]])
                    return j

                def pv(j_e, etile, betile, vrhs, bv, first, nrows=128, ncols=129):
                    for m in range(4):
                        K.op("pe", (lambda e, m=m: e.matmul(
                            psO[m // 3][:, (m % 3) * 161:(m % 3) * 161 + ncols], lhsT=etile[0:nrows, m * 128:(m + 1) * 128], rhs=vrhs,
                            start=(first and m % 3 == 0), stop=True, skip_group_check=True)), reads=[betile, bv], writes=[bpsO])

                def combine(branch, first, qb):
                    oi = osi[0] % 2
                    osi[0] += 1
                    ACC = ACCs[qb % 2]
                    bACC = bACCs[qb % 2]
                    O_ = OS[oi]
                    bO = bOS[oi]
                    K.op("act", (lambda e: e.copy(out=O_[:, 0, 0:483], in_=psO[0][:, 0:483])), reads=[bpsO], writes=[bO])
                    K.op("dve", (lambda e: e.tensor_copy(out=O_[:, 1, 0:161], in_=psO[1][:, 0:161])), reads=[bpsO], writes=[bO])
                    K.op("dve", (lambda e: e.tensor_scalar_add(out=RD[:, 0:3], in0=O_[:, 0, 0:483].rearrange("p (h c) -> p h c", c=161)[:, :, 128], scalar1=1e-30)),
                         reads=[bO], writes=[bRD])
                    K.op("dve", (lambda e: e.tensor_scalar_add(out=RD[:, 3:4], in0=O_[:, 1, 128:129], scalar1=1e-30)), reads=[bO], writes=[bRD])
                    K.op("dve", (lambda e: e.reciprocal(out=RD[:], in_=RD[:])), reads=[bRD], writes=[bRD])
                    K.op("dve", (lambda e: e.tensor_tensor(out=WG[:], in0=RD[:], in1=SG[qb % 2][:, :, branch], op=ALU.mult)),
                         reads=[bRD, bSG[qb % 2]], writes=[bWG])
                    for m in range(4):
                        o = O_[:, m // 3, (m % 3) * 161:(m % 3) * 161 + 128]
                        if first:
                            K.op("dve", (lambda e, m=m, o=o: e.tensor_scalar(out=ACC[:, m, :], in0=o, scalar1=WG[:, m:m + 1], scalar2=None, op0=ALU.mult)),
                                 reads=[bO, bWG], writes=[bACC])
                        else:
                            K.op("dve", (lambda e, m=m, o=o: e.scalar_tensor_tensor(out=ACC[:, m, :], in0=o, scalar=WG[:, m:m + 1], in1=ACC[:, m, :],
                                                                                    op0=ALU.mult, op1=ALU.add)), reads=[bO, bWG, bACC], writes=[bACC])
                    return O_, bO

                pairs = []
                deferred = []

                def mk_pre(qb):
                    def pre():
                        if qb % 4 == 0:
                            sgi = (qb // 4) % 2
                            for r in range(8):
                                r0 = OFF["nsag"] + (g * 8 + r) * 128
                                K.dma("sp" if r % 2 else "act", SGATE[sgi][:, r, :], self.projT[r0:r0 + 128, qb * 64:qb * 64 + 256], writes=[bSGATE[sgi]])
                        K.op("pe", (lambda e: e.transpose(out=ps_m[0][:, 0:128], in_=GLT[:, qb, :], identity=self.ident)),
                             reads=[bGLT, self.b_cf], writes=[bps_m[0]])
                        for r2 in range(2):
                            K.op("act", (lambda e, r2=r2: e.copy(
                                out=SG[qb % 2][r2 * 64:(r2 + 1) * 64, :, :],
                                in_=ps_m[0][r2 * 64:(r2 + 1) * 64, 0:24].rearrange("p (m r b) -> p m r b", r=2, b=3)[:, :, r2, :])),
                                reads=[bps_m[0]], writes=[bSG[qb % 2]])
                    return pre

                def mk_cmp(qb):
                    def score():
                        return score_pair(qb, KCMP[:, 0:127], [bKCMP], qb, 5, None, nrows=127)

                    def fin(j):
                        for r in range(8):
                            K.op("pool", (lambda e, r=r: e.affine_select(
                                out=s_sb2[0:127, r * 64:(r + 1) * 64], in_=s_sb[j][0:127, r * 64:(r + 1) * 64], pattern=[[1, 64]],
                                compare_op=ALU.is_ge, fill=fillreg(e), base=64 * qb - 31, channel_multiplier=-16)), reads=[bs_sb[j]], writes=[bs_sb2])
                        K.op("act", (lambda e: e.activation(out=ecf[0:127, :], in_=s_sb2[0:127, :], func=AF.Exp)), reads=[bs_sb2], writes=[becf])
                        pv(None, ecf, becf, VC1[0:127, :], bVC1, True, nrows=127, ncols=161)

                    def post():
                        O_, bO = combine(0, True, qb)
                        if qb >= 16:
                            for m in range(4):
                                u = O_[:, m // 3, (m % 3) * 161 + 129:(m % 3) * 161 + 161]
                                K.op("dve", (lambda e, m=m, u=u: e.tensor_scalar(out=UT[:, m, :], in0=u, scalar1=RD[:, m:m + 1], scalar2=None, op0=ALU.mult)),
                                     reads=[bO, bRD], writes=[bUT])
                            K.op("dve", (lambda e: e.tensor_reduce(out=IMP2[:], in_=UT[:].rearrange("p r n -> p n r"), axis=AX.X, op=ALU.add)),
                                 reads=[bUT], writes=[bIMP2])
                            K.op("pe", (lambda e: e.matmul(ps_m[1][:, 128:160], lhsT=FOLD[:], rhs=IMP2[:], start=True, stop=True)),
                                 reads=[bFOLD, bIMP2], writes=[bps_m[1]])
                            K.op("dve", (lambda e: e.tensor_copy(out=IMP[:], in_=ps_m[1][0:64, 128:160])), reads=[bps_m[1]], writes=[bIMP])
                            if qb < 31:
                                K.op("dve", (lambda e: e.memset(IMP[:, qb + 1:32], -1e30)), writes=[bIMP])
                            for col in (0, qb, qb - 1):
                                K.op("dve", (lambda e, col=col: e.memset(IMP[:, col:col + 1], 1e9)), writes=[bIMP])
                            K.op("dve", (lambda e: e.max(out=M8[:, 0:8], in_=IMP[:])), reads=[bIMP], writes=[bM8])
                            K.op("dve", (lambda e: e.match_replace(out=TMP[:], in_to_replace=M8[:, 0:8], in_values=IMP[:], imm_value=-3e38)),
                                 reads=[bIMP, bM8], writes=[bTMP])
                            K.op("dve", (lambda e: e.max(out=M8[:, 8:16], in_=TMP[:])), reads=[bTMP], writes=[bM8])
                            K.op("dve", (lambda e: e.tensor_scalar(out=MK[0:64, 0:32], in0=IMP[:], scalar1=M8[:, 15:16], scalar2=None, op0=ALU.is_ge)),
                                 reads=[bIMP, bM8], writes=[bMK])
                            K.op("dve", (lambda e: e.tensor_scalar(out=MK[0:64, 0:32], in0=MK[0:64, 0:32], scalar1=-1.0, scalar2=BIGS, op0=ALU.add, op1=ALU.mult)),
                                 reads=[bMK], writes=[bMK])
                            def later():
                                K.op("pe", (lambda e: e.transpose(out=ps_m[1][:, 0:128], in_=MK[:], identity=self.ident)),
                                     reads=[bMK, self.b_cf], writes=[bps_m[1]])
                                K.op("dve", (lambda e: e.tensor_copy(out=MBT[0:32, :, :], in_=ps_m[1][0:32, 0:64].unsqueeze(1).to_broadcast([32, 8, 64]))),
                                     reads=[bps_m[1]], writes=[bMBT])
                            deferred.append([2, later])
                    return dict(pre=mk_pre(qb), score=score, fin=fin, post=post, barrier=False, is_cmp=True)

                def mk_kv(qb, kt, delta, bidx, kT, bkT, V1, bV1, first, masked, post, barrier):
                    def score():
                        return score_pair(qb, kT[:, _cs(kt)], [bkT], delta, bidx, kt if masked else None)

                    def fin(j):
                        K.op("act", (lambda e: e.activation(out=eb[j][:], in_=s_sb[j][:], func=AF.Exp)), reads=[bs_sb[j]], writes=[beb[j]])
                        pv(j, eb[j], beb[j], V1[:, kt, :], bV1, first)
                    return dict(pre=None, score=score, fin=fin, post=post, barrier=barrier)

                def mk_final(qb):
                    def post():
                        combine(1, False, qb)
                        deferred.append([2, later])

                    def later():
                        ACC = ACCs[qb % 2]
                        bACC = bACCs[qb % 2]
                        for m in range(4):
                            K.op("pe", (lambda e, m=m: e.transpose(out=ps_y[:, m * 128:(m + 1) * 128], in_=ACC[:, m, :], identity=self.ident)),
                                 reads=[bACC, self.b_cf], writes=[bps_y])
                        sgi = (qb // 4) % 2
                        yi = qb % 2
                        K.op("dve", (lambda e: e.tensor_tensor(
                            out=Yt[yi][:], in0=ps_y[:, :].rearrange("p (r t) -> p r t", t=64),
                            in1=SGATE[sgi][:, :, (qb % 4) * 64:(qb % 4 + 1) * 64], op=ALU.mult)), reads=[bps_y, bSGATE[sgi]], writes=[bYt[yi]])
                        r0 = (8 + g * 8) * 128
                        K.dma("sp", self.yT[r0:r0 + 1024, qb * 64:(qb + 1) * 64].rearrange("(r p) t -> p r t", p=128), Yt[yi][:], reads=[bYt[yi]])
                    return post

                for qb in range(32):
                    pairs.append(mk_cmp(qb))
                    deltas = [d for d in range(qb % 2, 10, 2) if (qb - d) >= 0]
                    rd = list(reversed(deltas))
                    for n_, delta in enumerate(rd):
                        kt = (qb - delta) // 2
                        bidx = {0: 1, 1: 2, 8: 3, 9: 4}.get(delta, 0)
                        post = (lambda qb=qb: combine(2, False, qb)) if n_ == len(rd) - 1 else None
                        pairs.append(mk_kv(qb, kt, delta, bidx, kTw, bkTw, V1w, bV1w, n_ == 0, False, post, barrier=False))
                    nkt = qb // 2 + 1
                    for kt in range(nkt):
                        delta = qb - 2 * kt
                        bidx = 1 if delta == 0 else (2 if delta == 1 else 0)
                        post = mk_final(qb) if kt == nkt - 1 else None
                        pairs.append(mk_kv(qb, kt, delta, bidx, kTs, bkTs, V1s, bV1s, kt == 0, qb >= 16, post, barrier=(kt == 0 and qb >= 16)))

                pend = []

                def run_deferred(force=False):
                    keep = []
                    for item in deferred:
                        item[0] -= 1
                        if force or item[0] <= 0:
                            item[1]()
                        else:
                            keep.append(item)
                    deferred[:] = keep

                def finish(pd):
                    p_, j_ = pd
                    p_["fin"](j_)
                    if p_["post"] is not None:
                        p_["post"]()

                for p_ in pairs:
                    if p_["barrier"]:
                        while pend and any(q_[0]["post"] is not None and q_[0].get("is_cmp") for q_ in pend):
                            finish(pend.pop(0))
                        run_deferred(force=True)
                    if p_["pre"] is not None:
                        p_["pre"]()
                    j_ = p_["score"]()
                    pend.append((p_, j_))
                    if len(pend) > 2:
                        finish(pend.pop(0))
                    run_deferred()
                while pend:
                    finish(pend.pop(0))
                run_deferred(force=True)
            self.end_phase()

    def phase3(self, l, xsrc, xdst, bxdst):
        K = self.K
        P = self.prm_sb
        with ExitStack() as es:
            K.les = es
            acc1 = K.sb("acc1", [128, T]); bacc1 = Buf()
            acc2 = K.sb("acc2", [128, T]); bacc2 = Buf()
            with ExitStack() as es2:
                K.les = es2
                big = K.sb("big", [128, NKC, T], BF16)
                bbig = Buf()
                for kc in range(NKC):
                    K.dma("sp" if kc % 2 else "act", big[:, kc, :], self.yT[_cs(kc), :], writes=[bbig])
                xr = [K.sb("xr", [128, 512]) for _ in range(3)]; bxr = [Buf() for _ in range(3)]
                vt = [K.sb("vt", [128, 512]) for _ in range(3)]; bvt = [Buf() for _ in range(3)]
                sq = [K.sb("sq", [128, 512]) for _ in range(2)]; bsq = [Buf() for _ in range(2)]
                K.op("pool", lambda e: e.memset(acc1[:], 0.0), writes=[bacc1])
                K.op("pool", lambda e: e.memset(acc2[:], 0.0), writes=[bacc2])
                cnt = [0]

                def consume(ct, tb, ps, bps):
                    k = cnt[0] % 3
                    k2 = cnt[0] % 2
                    cnt[0] += 1
                    ts = slice(tb * 512, (tb + 1) * 512)
                    K.dma("sp", xr[k][:], xsrc[_cs(ct), ts], writes=[bxr[k]])
                    K.op("act", (lambda e, k=k: e.mul(out=xr[k][:], in_=xr[k][:], mul=ALPHA)), reads=[bxr[k]], writes=[bxr[k]])
                    K.op("dve", (lambda e, k=k, ps=ps, ct=ct: e.scalar_tensor_tensor(
                        out=vt[k][:], in0=ps[:, :], scalar=self.modT[:, l, 64 + ct:65 + ct], in1=xr[k][:], op0=ALU.mult, op1=ALU.add)),
                        reads=[bps, bxr[k], self.b_mod], writes=[bvt[k]])
                    K.op("pool", (lambda e, k=k, ts=ts: e.tensor_tensor(out=acc1[:, ts], in0=acc1[:, ts], in1=vt[k][:], op=ALU.add)),
                         reads=[bvt[k], bacc1], writes=[bacc1])
                    K.op("act", (lambda e, k=k, k2=k2: e.activation(out=sq[k2][:], in_=vt[k][:], func=AF.Square)), reads=[bvt[k]], writes=[bsq[k2]])
                    K.op("pool", (lambda e, k2=k2, ts=ts: e.tensor_tensor(out=acc2[:, ts], in0=acc2[:, ts], in1=sq[k2][:], op=ALU.add)),
                         reads=[bsq[k2], bacc2], writes=[bacc2])
                    K.dma("act", self.vT[_cs(ct), ts], vt[k][:], reads=[bvt[k]], writes=[self.b_vT])

                self._proj_loop(big, bbig, self.wout, l * 32, 32, consume, n_wst=2)
                self.end_phase()
            with ExitStack() as es3:
                K.les = es3
                ps = [K.ps("lnp", [128, 512]) for _ in range(4)]; bps = [Buf() for _ in range(4)]
                msq = K.sb("msq", [128, T]); bmsq = Buf()
                for tb in range(4):
                    ts = slice(tb * 512, (tb + 1) * 512)
                    K.op("pe", (lambda e, tb=tb, ts=ts: e.matmul(ps[tb][:, :], lhsT=self.ones, rhs=acc1[:, ts], start=True, stop=True)),
                         reads=[self.b_cf, bacc1], writes=[bps[tb]])
                    K.op("dve", (lambda e, tb=tb, ts=ts: e.tensor_scalar_mul(out=acc1[:, ts], in0=ps[tb][:, :], scalar1=1.0 / D)),
                         reads=[bps[tb]], writes=[bacc1])
                K.op("dve", (lambda e: e.tensor_tensor(out=msq[:], in0=acc1[:], in1=acc1[:], op=ALU.mult)), reads=[bacc1], writes=[bmsq])
                for tb in range(4):
                    ts = slice(tb * 512, (tb + 1) * 512)
                    K.op("pe", (lambda e, tb=tb, ts=ts: e.matmul(ps[tb][:, :], lhsT=self.ones, rhs=acc2[:, ts], start=True, stop=True)),
                         reads=[self.b_cf, bacc2], writes=[bps[tb]])
                    K.op("dve", (lambda e, tb=tb, ts=ts: e.scalar_tensor_tensor(out=acc2[:, ts], in0=ps[tb][:, :], scalar=1.0 / D, in1=msq[:, ts],
                                                                            op0=ALU.mult, op1=ALU.subtract)), reads=[bps[tb], bmsq], writes=[bacc2])
                K.op("act", (lambda e: e.activation(out=acc2[:], in_=acc2[:], func=AF.Sqrt, bias=self.eps_ap)), reads=[bacc2], writes=[bacc2])
                K.op("dve", (lambda e: e.reciprocal(out=acc2[:], in_=acc2[:])), reads=[bacc2], writes=[bacc2])
                vin = [K.sb("vin", [128, T]) for _ in range(2)]; bvin = [Buf() for _ in range(2)]
                xo = [K.sb("xo", [128, T]) for _ in range(2)]; bxo = [Buf() for _ in range(2)]
                for ct in range(32):
                    i = ct % 2
                    K.dma("act", vin[i][:], self.vT[_cs(ct), :], reads=[self.b_vT], writes=[bvin[i]])
                    K.op("dve", (lambda e, i=i: e.tensor_tensor(out=vin[i][:], in0=vin[i][:], in1=acc1[:], op=ALU.subtract)),
                         reads=[bvin[i], bacc1], writes=[bvin[i]])
                    K.op("pool", (lambda e, i=i: e.tensor_tensor(out=vin[i][:], in0=vin[i][:], in1=acc2[:], op=ALU.mult)),
                         reads=[bvin[i], bacc2], writes=[bvin[i]])
                    K.op("act", (lambda e, i=i, ct=ct: e.activation(out=xo[i][:], in_=vin[i][:], func=AF.Identity,
                                                                   scale=P[:, l, P_LNG + ct:P_LNG + ct + 1], bias=P[:, l, P_LNB + ct:P_LNB + ct + 1])),
                         reads=[bvin[i], self.b_prm], writes=[bxo[i]])
                    K.dma("sp", xdst[_cs(ct), :], xo[i][:], reads=[bxo[i]], writes=[bxdst])
                self.end_phase()


def _bf16_split3(x):
    x = np.asarray(x, np.float64)
    h = x.astype(np.float32).astype(ml_dtypes.bfloat16)
    r1 = x - h.astype(np.float64)
    m = r1.astype(np.float32).astype(ml_dtypes.bfloat16)
    r2 = r1 - m.astype(np.float64)
    lo = r2.astype(np.float32).astype(ml_dtypes.bfloat16)
    return h, m, lo


def make_consts():
    cf = np.zeros((128, 352), np.float32)
    for p_ in range(128):
        cf[p_, 288 + p_ % 64] = 1.0
    cf[:, 0:128] = np.eye(128, dtype=np.float32)
    cf[:, 128:256] = 1.0
    n_cmp, n_sel = 127, 32
    s_c = np.arange(n_cmp) * 16
    s_s = np.arange(n_sel) * 64
    ov = np.clip(np.minimum(s_c[:, None] + 32, s_s[None, :] + 64) - np.maximum(s_c[:, None], s_s[None, :]), 0, None).astype(np.float32) / 32
    cf[0:127, 256:288] = ov
    h = np.arange(1, 17, dtype=np.float32)
    slopes = np.power(np.float32(2.0), -8.0 * h / 16).astype(np.float32).reshape(2, 8).astype(np.float64)
    jl = np.arange(128)[:, None, None].astype(np.float64)
    tl = np.arange(64)[None, None, :].astype(np.float64)
    cbt = np.zeros((2, 128, 6, 8, 64), np.float32)
    cb = np.zeros((128, 7168), ml_dtypes.bfloat16)
    for g in range(2):
        s = slopes[g][None, :, None]
        d = tl - jl
        gen = -s * d
        cbt[g, :, 0] = gen
        cbt[g, :, 1] = np.where(d >= 0, gen, NEG)
        cbt[g, :, 2] = np.where(64 + d >= 0, gen, NEG)
        cbt[g, :, 3] = np.where(d < 0, gen, NEG)
        cbt[g, :, 4] = np.where(64 + d < 0, gen, NEG)
        cbt[g, :, 5] = -s * (tl - 16 * jl - 31)
        val = -slopes[g] * 64.0 / SCALE
        parts = _bf16_split3(val)
        sc = np.zeros((24, 8, 64), ml_dtypes.bfloat16)
        for pi, pr in enumerate(parts):
            for r in range(8):
                sc[pi * 8 + r, r, :] = pr[r]
        cb[0:24, 4096 + g * 512:4096 + (g + 1) * 512] = sc.reshape(24, 512)
    ld = np.zeros((24, 32, 128), np.float32)
    ld[:] = np.arange(32, dtype=np.float32)[None, :, None]
    cb[0:24, 0:4096] = ld.reshape(24, 4096).astype(ml_dtypes.bfloat16)
    ex = np.zeros((32, 16, 128), np.float32)
    for kt in range(16):
        for j in range(128):
            ex[2 * kt + j // 64, kt, j] = 1.0
    cb[0:32, 5120:7168] = ex.reshape(32, 2048).astype(ml_dtypes.bfloat16)
    s_ = np.arange(128)[:, None]
    t_ = np.arange(128)[None, :]
    cmask = (t_ >= s_).astype(np.int32)
    return dict(cf=cf, cbt=np.ascontiguousarray(cbt.reshape(2, 128, 3072)), cb=cb, cmask=cmask)


def _tile_w(w, ncol_tiles):
    a = w.reshape(NKC, 128, ncol_tiles, 128)
    return np.ascontiguousarray(a.transpose(2, 1, 0, 3)).reshape(ncol_tiles, 128, NKC * 128)


def prep_shared(inp):
    f = np.float32
    w_in = np.asarray(inp["w_in"], f)
    win = []
    for l in range(2):
        w = w_in[l]
        wp = np.concatenate([w[:, :5632], w[:, 5680:], w[:, 5632:5680], np.zeros((D, NPROJ - 11824), f)], axis=1)
        win.append(_tile_w(wp, NCT_IN))
    win = np.concatenate(win, axis=0)
    def _tile_ada(w):
        a = w.reshape(4, 8, 128, 24, 512)
        return np.ascontiguousarray(a.transpose(3, 0, 2, 1, 4)).reshape(96, 128, 4096)
    wada = np.concatenate([_tile_ada(np.asarray(inp["w_ada"][l], f)) for l in range(2)], axis=0)
    wout = np.concatenate([_tile_w(np.asarray(inp["w_out"][l], f), 32) for l in range(2)], axis=0)
    prm = np.zeros((128, 2, NPRM), f)

    def pt(v, n):
        return np.asarray(v, f).reshape(n, 128).T

    for l in range(2):
        cw = np.asarray(inp["rg_conv_w"][l], f)
        prm[:, l, P_CONVW:P_CONVW + 32] = cw.reshape(4, 8, 128).transpose(2, 1, 0).reshape(128, 32)
        prm[:, l, P_CONVB:P_CONVB + 8] = pt(inp["rg_conv_b"][l], 8)
        prm[:, l, P_BA:P_BA + 8] = pt(inp["rg_b_a"][l], 8)
        prm[:, l, P_BX:P_BX + 8] = pt(inp["rg_b_x"][l], 8)
        prm[:, l, P_LAM:P_LAM + 8] = pt(inp["rg_lambda"][l], 8)
        for l2 in range(2):
            prm[:, l, P_LB + l2 * 8:P_LB + l2 * 8 + 8] = pt(inp["hg_lower_bounds"][l2], 8)
        prm[:, l, P_NG] = np.asarray(inp["hg_norm_g"][l], f)
        prm[:, l, P_LNG:P_LNG + 32] = pt(inp["ln_g"][l], 32)
        prm[:, l, P_LNB:P_LNB + 32] = pt(inp["ln_b"][l], 32)
        prm[:, l, P_BADA:P_BADA + 96] = pt(inp["b_ada"][l], 96)
        prm[:, l, P_PEK:P_PEK + 32] = np.asarray(inp["nsa_pe_k"][l], f).T
        prm[:, l, P_PEV:P_PEV + 32] = np.asarray(inp["nsa_pe_v"][l], f).T
    rgw = np.zeros((2, 128, 2, 8, 128), f)
    cw1 = np.zeros((2, 2, 128, 32, 256), f)
    cw2 = np.zeros((2, 128, 2, 2, 128), f)
    for l in range(2):
        rgw[l, :, 0] = np.asarray(inp["rg_w_a"][l], f).transpose(1, 0, 2)
        rgw[l, :, 1] = np.asarray(inp["rg_w_x"][l], f).transpose(1, 0, 2)
        for xi, (k1, k2) in enumerate((("nsa_cmp_w1_k", "nsa_cmp_w2_k"), ("nsa_cmp_w1_v", "nsa_cmp_w2_v"))):
            cw1[l, xi] = np.asarray(inp[k1][l], f).reshape(32, 128, 256).transpose(1, 0, 2)
            cw2[l, :, xi] = np.asarray(inp[k2][l], f).reshape(2, 128, 128).transpose(1, 0, 2)
    sh = dict(wada=wada, win=win, wout=wout, prm=prm, rgw=rgw.reshape(2, 128, 2048),
              cw1=cw1.reshape(2, 2, 128, 8192), cw2=cw2.reshape(2, 128, 512))
    sh.update(make_consts())
    return sh


def prep_core(inp, b):
    x = np.asarray(inp["x"], np.float32)
    c = np.asarray(inp["c"], np.float32)
    xT = np.ascontiguousarray(x[b].T)
    cT = np.repeat(c[b].reshape(NKC, 128).T[:, :, None], 2, axis=2).reshape(128, 64)
    return dict(xT=xT, cT=np.ascontiguousarray(cT))


def build_program(**kw):
    nc = bass.Bass("TRN2", target_bir_lowering=False)
    p = Prog(nc, **kw)
    p.eps_ap = LN_EPS
    p.build()
    return nc


def kernel(**inputs):
    sh = prep_shared(inputs)
    nc = build_program()
    ACTIVE = (0, 1, 4, 5)
    zeros = None
    in_maps = []
    for core in range(8):
        if core in ACTIVE:
            m = dict(sh)
            m.update(prep_core(inputs, ACTIVE.index(core)))
        else:
            if zeros is None:
                zeros = {k: np.zeros_like(v) for k, v in in_maps[0].items()}
            m = zeros
        in_maps.append(m)
    res = run_bass_kernel_spmd(nc, in_maps, core_ids=list(range(8)))
    out = np.stack([np.asarray(res.results[c]["outT"], np.float32).T for c in ACTIVE], axis=0)
    return np.ascontiguousarray(out)
```

```python
from contextlib import ExitStack
import numpy as np
import ml_dtypes
import concourse.bass as bass
import concourse.mybir as mybir
from concourse.bass_utils import run_bass_kernel_spmd

F32 = mybir.dt.float32
BF16 = mybir.dt.bfloat16
I32 = mybir.dt.int32
AF = mybir.ActivationFunctionType
ALU = mybir.AluOpType
AX = mybir.AxisListType

T = 2048
D = 4096
DEPTH = 2
NKC = 32
NCT_IN = 93
NPROJ = NCT_IN * 128
OFF = dict(rgx=0, rgg=1024, q=2048, kvc=4096, kvs=4608, kvw=5120, nsag=5632,
           hq=7680, hf=8704, hi=9728, hg=10752, gl=11776)
ALPHA = (2.0 * DEPTH) ** 0.25
LN_EPS = 1e-5
RMS_EPS = 1e-6
SCALE = 128 ** -0.5
NEG = -30000.0
BIGS = 30000.0 / SCALE
P_CONVW, P_CONVB, P_BA, P_BX, P_LAM, P_LB, P_NG, P_LNG, P_LNB, P_BADA, P_PEK, P_PEV = (
    0, 32, 40, 48, 56, 64, 80, 81, 113, 145, 241, 273)
NPRM = 305

ENGS = ("pe", "dve", "act", "pool", "sp")
SEM_ROLL = 30000
N_DMA_SEMS = 40


class Buf:
    __slots__ = ("name", "w", "r")

    def __init__(self, name=""):
        self.name = name
        self.w = None
        self.r = {}


class KB:
    def __init__(self, nc, es):
        self.nc = nc
        self.es = es
        self.les = es
        self.q = {e: [] for e in ENGS}
        self.eng_sem = {}
        self.eng_cnt = {}
        self.eng_last = {}
        self.waited = {e: {} for e in ENGS}
        self.semn = 0
        self.dma_sems = []
        self.dma_cnt = []
        self.dma_last = []
        self.dma_i = 0
        self.nalloc = 0
        self.yield_hook = None

    def newsem(self, name):
        self.semn += 1
        return self.es.enter_context(self.nc.semaphore(f"{name}_{self.semn}"))

    def sb(self, name, shape, dtype=F32, glob=False):
        self.nalloc += 1
        es = self.es if glob else self.les
        return es.enter_context(self.nc.sbuf_tensor(f"{name}_{self.nalloc}", list(shape), dtype))

    def ps(self, name, shape, dtype=F32):
        self.nalloc += 1
        return self.les.enter_context(self.nc.psum_tensor(f"{name}_{self.nalloc}", list(shape), dtype))

    def _tick(self, eng):
        if eng not in self.eng_sem or self.eng_cnt[eng] >= SEM_ROLL:
            self.eng_sem[eng] = self.newsem("s" + eng)
            self.eng_cnt[eng] = 0
        self.eng_cnt[eng] += 1
        ev = (self.eng_sem[eng], self.eng_cnt[eng], eng)
        self.eng_last[eng] = ev
        return ev

    def _dma_event(self):
        if len(self.dma_sems) < N_DMA_SEMS:
            self.dma_sems.append(self.newsem("dma"))
            self.dma_cnt.append(0)
            self.dma_last.append(None)
        i = self.dma_i % N_DMA_SEMS
        self.dma_i += 1
        prev = self.dma_last[i]
        self.dma_cnt[i] += 16
        ev = (self.dma_sems[i], self.dma_cnt[i], "dma")
        self.dma_last[i] = ev
        return ev, prev

    @staticmethod
    def _deps(reads, writes):
        w = []
        for b in reads:
            if b.w is not None:
                w.append(b.w)
        for b in writes:
            if b.w is not None:
                w.append(b.w)
            w.extend(b.r.values())
        return w

    def _filter(self, eng, waits):
        wd = self.waited[eng]
        best = {}
        for ev in waits:
            sem, val, src = ev
            if eng == "pe" and src == "pe":
                continue
            k = id(sem)
            if wd.get(k, 0) >= val:
                continue
            if k not in best or best[k][1] < val:
                best[k] = ev
        out = []
        for k, ev in best.items():
            wd[k] = ev[1]
            out.append((ev[0], ev[1]))
        return out

    @staticmethod
    def _commit(ev, reads, writes):
        k = id(ev[0])
        for b in reads:
            o = b.r.get(k)
            if o is None or o[1] < ev[1]:
                b.r[k] = ev
        for b in writes:
            b.w = ev
            b.r = {}

    def op(self, eng, fn, reads=(), writes=(), extra=()):
        waits = self._filter(eng, self._deps(reads, writes) + list(extra))
        ev = self._tick(eng)
        self.q[eng].append((waits, fn, (ev[0], 1)))
        self._commit(ev, reads, writes)
        if self.yield_hook is not None:
            self.yield_hook()
        return ev

    def dma(self, eng, out, in_, reads=(), writes=(), extra=(), **kw):
        ev, prev = self._dma_event()
        deps = self._deps(reads, writes) + list(extra)
        if prev is not None:
            deps.append(prev)
        waits = self._filter(eng, deps)
        self.q[eng].append((waits, lambda e: e.dma_start(out=out, in_=in_, **kw), (ev[0], 16)))
        self._commit(ev, reads, writes)
        return ev

    def wait_all(self, eng, events):
        waits = self._filter(eng, list(events))
        if waits:
            self.q[eng].append((waits, None, None))

    def barrier(self):
        evs = [e for e in self.dma_last if e is not None] + list(self.eng_last.values())
        for e in ENGS:
            self.wait_all(e, evs)

    def flush(self, block):
        def mk(name):
            lst = self.q[name]

            def body(eng):
                for waits, fn, sig in lst:
                    for sem, val in waits:
                        eng.wait_ge(sem, val)
                    if fn is None:
                        continue
                    ins = fn(eng)
                    if sig is not None:
                        ins.then_inc(sig[0], sig[1])
            return body
        for name, starter in (("sp", block.sync), ("pe", block.tensor), ("dve", block.vector),
                              ("act", block.scalar), ("pool", block.gpsimd)):
            if self.q[name]:
                starter(mk(name))
        self.q = {e: [] for e in ENGS}


def _interleave(K, fns):
    import threading
    n = len(fns)
    sems = [threading.Semaphore(0) for _ in range(n)]
    alive = [True] * n
    done = threading.Semaphore(0)
    errs = []
    cur = [0]

    def nxt(i):
        for d in range(1, n + 1):
            j = (i + d) % n
            if alive[j]:
                return j
        return None

    def hook():
        i = cur[0]
        j = nxt(i)
        if j is None or j == i:
            return
        cur[0] = j
        sems[j].release()
        sems[i].acquire()

    def runner(i):
        sems[i].acquire()
        try:
            fns[i]()
        except BaseException as ex:
            errs.append(ex)
        alive[i] = False
        j = nxt(i)
        if j is None:
            done.release()
        else:
            cur[0] = j
            sems[j].release()

    ths = [threading.Thread(target=runner, args=(i,)) for i in range(n)]
    for t in ths:
        t.start()
    K.yield_hook = hook
    cur[0] = 0
    sems[0].release()
    done.acquire()
    K.yield_hook = None
    for t in ths:
        t.join()
    if errs:
        raise errs[0]


def _cs(i, n=128):
    return slice(i * n, (i + 1) * n)


class Prog:
    def __init__(self, nc, stop_after=None, layers=(0, 1), dbg=False):
        self.nc = nc
        self.stop_after = stop_after
        self.layers = layers
        self.dbg = dbg

    def dram_in(self, name, shape, dt=F32):
        return self.nc.dram_tensor(name, list(shape), dt, kind="ExternalInput").ap()

    def build(self):
        nc = self.nc
        self.xT = self.dram_in("xT", [D, T])
        self.cT = self.dram_in("cT", [128, 64])
        self.wada = self.dram_in("wada", [2 * 96, 128, 4096])
        self.win = self.dram_in("win", [2 * NCT_IN, 128, 4096])
        self.wout = self.dram_in("wout", [2 * 32, 128, 4096])
        self.prm = self.dram_in("prm", [128, 2, NPRM])
        self.rgw = self.dram_in("rgw", [2, 128, 2 * 8 * 128])
        self.cw1 = self.dram_in("cw1", [2, 2, 128, 32 * 256])
        self.cw2 = self.dram_in("cw2", [2, 128, 2 * 256])
        self.cf = self.dram_in("cf", [128, 352])
        self.cbt = self.dram_in("cbt", [2, 128, 6 * 512])
        self.cmask = self.dram_in("cmask", [128, 128], I32)
        self.cb = self.dram_in("cb", [128, 7168], BF16)
        okind = "ExternalOutput"
        self.outT = nc.dram_tensor("outT", [D, T], F32, kind=okind).ap()
        skind = "ExternalOutput" if self.dbg else "Internal"
        self.projT = nc.dram_tensor("projT", [NPROJ, T], F32, kind=skind).ap()
        self.yT = nc.dram_tensor("yT", [D, T], BF16, kind=skind).ap()
        self.vT = nc.dram_tensor("vT", [D, T], F32, kind="Internal").ap()
        self.x1T = nc.dram_tensor("x1T", [D, T], F32, kind=skind).ap()
        self.b_projT = Buf("projT")
        self.b_yT = Buf("yT")
        self.b_vT = Buf("vT")
        self.b_x1T = Buf("x1T")
        self.b_outT = Buf("outT")

        with ExitStack() as ges, nc.Block() as block:
            K = KB(nc, ges)
            self.K = K
            self.block = block
            self.prm_sb = K.sb("prm", [128, 2, NPRM], F32, glob=True)
            self.b_prm = Buf()
            K.dma("sp", self.prm_sb[:], self.prm[:, :, :], writes=[self.b_prm])
            self.cf_sb = K.sb("cf", [128, 352], F32, glob=True)
            self.b_cf = Buf()
            K.dma("sp", self.cf_sb[:], self.cf[:, :], writes=[self.b_cf])
            self.ident = self.cf_sb[:, 0:128]
            self.ones = self.cf_sb[:, 128:256]
            self.modT = K.sb("modT", [128, 2, 96], F32, glob=True)
            self.b_mod = Buf()
            self.misc = K.sb("misc", [128, 64], F32, glob=True)
            self.b_misc = Buf()

            self.phase0()
            done = self.stop_after == "p0"
            for l in self.layers:
                if done:
                    break
                xsrc, bxsrc = (self.xT, None) if l == 0 else (self.x1T, self.b_x1T)
                xdst, bxdst = (self.x1T, self.b_x1T) if l == 0 else (self.outT, self.b_outT)
                self.phase1(l, xsrc)
                if self.stop_after == f"p1_{l}":
                    break
                self.phase2_rghg(l)
                if self.stop_after == f"p2c_{l}":
                    break
                self.phase2_nsa(l)
                if self.stop_after == f"p2b_{l}":
                    break
                self.phase3(l, xsrc, xdst, bxdst)
                if self.stop_after == f"p3_{l}":
                    break
            K.barrier()
            K.flush(block)

    def end_phase(self):
        self.K.barrier()
        self.K.flush(self.block)

    def phase0(self):
        K = self.K
        with ExitStack() as es:
            K.les = es
            NS = 6
            wst = [K.sb("p0w", [128, 4096]) for _ in range(NS)]
            wb = [Buf() for _ in range(NS)]
            cT = K.sb("cT", [128, 64])
            bc = Buf()
            K.dma("sp", cT[:], self.cT[:, :], writes=[bc])
            cTb = K.sb("cTb", [128, 64], BF16)
            K.op("dve", (lambda e: e.tensor_copy(out=cTb[:], in_=cT[:])), reads=[bc], writes=[bc])
            wbf0 = [K.sb("p0wb", [128, 4096], BF16) for _ in range(2)]
            bwbf0 = [[Buf(), Buf()] for _ in range(2)]
            psr = [K.ps("psrow", [128, 512]) for _ in range(2)]
            bpsr = [Buf() for _ in range(2)]
            pst = K.ps("pst", [128, 512])
            bpst = Buf()
            ROW = K.sb("ROW", [2, 12288])
            bROW = Buf()
            for l in range(2):
                for cbk in range(24):
                    pb = cbk % 2
                    for q in range(4):
                        g = l * 96 + cbk * 4 + q
                        i = g % NS
                        K.dma("sp", wst[i][:], self.wada[g, :, :], writes=[wb[i]])
                        i2 = g % 2
                        for hf in range(2):
                            if hf == 0:
                                K.op("dve", (lambda e, i=i, i2=i2, hf=hf: e.tensor_copy(out=wbf0[i2][:, hf * 2048:(hf + 1) * 2048], in_=wst[i][:, hf * 2048:(hf + 1) * 2048])),
                                     reads=[wb[i]], writes=[bwbf0[i2][hf]])
                            else:
                                K.op("act", (lambda e, i=i, i2=i2, hf=hf: e.copy(out=wbf0[i2][:, hf * 2048:(hf + 1) * 2048], in_=wst[i][:, hf * 2048:(hf + 1) * 2048])),
                                     reads=[wb[i]], writes=[bwbf0[i2][hf]])
                        for k8 in range(8):
                            kc = q * 8 + k8
                            K.op("pe", (lambda e, i2=i2, kc=kc, k8=k8, pb=pb: e.matmul(
                                psr[pb][0:2, :], lhsT=cTb[:, 2 * kc:2 * kc + 2], rhs=wbf0[i2][:, k8 * 512:(k8 + 1) * 512],
                                start=(kc == 0), stop=(kc == NKC - 1))), reads=[bwbf0[i2][k8 // 4], bc], writes=[bpsr[pb]])
                    K.op("act", (lambda e, pb=pb, cbk=cbk: e.copy(out=ROW[0:2, cbk * 512:(cbk + 1) * 512], in_=psr[pb][0:2, :])),
                         reads=[bpsr[pb]], writes=[bROW])
                for nt in range(96):
                    K.op("pe", (lambda e, nt=nt: e.transpose(out=pst[:, 2 * nt:2 * nt + 2], in_=ROW[0:2, _cs(nt)], identity=self.ident[0:2, 0:2])),
                         reads=[bROW, self.b_cf], writes=[bpst])
                src = pst[:, 0:192].rearrange("p (n two) -> p n two", two=2)[:, :, 0]
                K.op("dve", (lambda e, l=l, src=src: e.tensor_tensor(
                    out=self.modT[:, l, :], in0=src, in1=self.prm_sb[:, l, P_BADA:P_BADA + 96], op=ALU.add)),
                    reads=[bpst, self.b_prm], writes=[self.b_mod])
                K.op("dve", (lambda e, l=l: e.tensor_scalar_add(
                    out=self.modT[:, l, 32:96], in0=self.modT[:, l, 32:96], scalar1=1.0)),
                    reads=[self.b_mod], writes=[self.b_mod])
            m = self.misc
            for l in range(2):
                K.op("act", (lambda e, l=l: e.activation(out=m[:, 48 + 0:48 + 8], in_=self.prm_sb[:, l, P_LAM:P_LAM + 8],
                                                          func=AF.Exp, scale=-1.0)), reads=[self.b_prm], writes=[self.b_misc])
                K.op("act", (lambda e: e.activation(out=m[:, 48:56], in_=m[:, 48:56], func=AF.Ln, bias=1.0)),
                     reads=[self.b_misc], writes=[self.b_misc])
                K.op("dve", (lambda e, l=l: e.tensor_scalar_mul(out=m[:, l * 16:l * 16 + 8], in0=m[:, 48:56], scalar1=-8.0)),
                     reads=[self.b_misc], writes=[self.b_misc])
                K.op("dve", (lambda e, l=l: e.tensor_scalar_mul(out=m[:, l * 16 + 8:l * 16 + 16], in0=m[:, 48:56], scalar1=-16.0)),
                     reads=[self.b_misc], writes=[self.b_misc])
            K.op("dve", (lambda e: e.memset(m[:, 32:40], 1.0)), writes=[self.b_misc])
            K.op("dve", (lambda e: e.tensor_tensor(out=m[:, 56:64], in0=self.prm_sb[:, 0, P_LB:P_LB + 8],
                                                   in1=self.prm_sb[:, 0, P_LB + 8:P_LB + 16], op=ALU.subtract)),
                 reads=[self.b_prm], writes=[self.b_misc])
            K.op("act", (lambda e: e.activation(out=m[:, 40:48], in_=m[:, 56:64], func=AF.Sigmoid)),
                 reads=[self.b_misc], writes=[self.b_misc])
            self.end_phase()

    def _proj_loop(self, big, bbig, wsrc, wbase, nct, consume, n_wst=3):
        K = self.K
        wst = [K.sb("wst", [128, 2048]) for _ in range(n_wst)]
        bwst = [Buf() for _ in range(n_wst)]
        wbf = [K.sb("wbf", [128, NKC, 128], BF16) for _ in range(2)]
        bwbf = [Buf() for _ in range(2)]
        ps = [K.ps("pj", [128, 512]) for _ in range(8)]
        bps = [Buf() for _ in range(8)]
        hi = [0]

        def load(ct):
            s = ct % 2
            for half in range(2):
                j = hi[0] % n_wst
                hi[0] += 1
                K.dma("sp", wst[j][:], wsrc[wbase + ct, :, half * 2048:(half + 1) * 2048], writes=[bwst[j]])
                K.op("dve", (lambda e, j=j, s=s, half=half: e.tensor_copy(
                    out=wbf[s][:, half * 16:(half + 1) * 16, :],
                    in_=wst[j][:, :].rearrange("p (k c) -> p k c", c=128))),
                    reads=[bwst[j]], writes=[bwbf[s]])

        load(0)
        for ct in range(nct):
            s = ct % 2
            if ct + 1 < nct:
                load(ct + 1)
            pp = (ct % 2) * 4
            for kc in range(NKC):
                for tb in range(4):
                    K.op("pe", (lambda e, s=s, kc=kc, tb=tb, pp=pp: e.matmul(
                        ps[pp + tb][:, :], lhsT=wbf[s][:, kc, :], rhs=big[:, kc, tb * 512:(tb + 1) * 512],
                        start=(kc == 0), stop=(kc == NKC - 1))),
                        reads=[bwbf[s], bbig], writes=[bps[pp + tb]])
            for tb in range(4):
                consume(ct, tb, ps[pp + tb], bps[pp + tb])

    def phase1(self, l, xsrc):
        K = self.K
        with ExitStack() as es:
            K.les = es
            big = K.sb("big", [128, NKC, T], BF16)
            bbig = Buf()
            xst = [K.sb("xst", [128, 1024]) for _ in range(2)]
            bxst = [Buf() for _ in range(2)]
            n = 0
            for kc in range(NKC):
                for hf in range(2):
                    i = n % 2
                    n += 1
                    K.dma("act" if n % 2 else "sp", xst[i][:], xsrc[_cs(kc), hf * 1024:(hf + 1) * 1024], writes=[bxst[i]])
                    if hf == 0:
                        K.op("act", (lambda e, i=i, kc=kc, hf=hf: e.activation(
                            out=big[:, kc, hf * 1024:(hf + 1) * 1024], in_=xst[i][:], func=AF.Identity,
                            scale=self.modT[:, l, 32 + kc:33 + kc], bias=self.modT[:, l, kc:kc + 1])),
                            reads=[bxst[i], self.b_mod], writes=[bbig])
                    else:
                        K.op("dve", (lambda e, i=i, kc=kc, hf=hf: e.tensor_scalar(
                            out=big[:, kc, hf * 1024:(hf + 1) * 1024], in0=xst[i][:], scalar1=self.modT[:, l, 32 + kc:33 + kc],
                            scalar2=self.modT[:, l, kc:kc + 1], op0=ALU.mult, op1=ALU.add)),
                            reads=[bxst[i], self.b_mod], writes=[bbig])
            ev = [K.sb("ev", [128, 512]) for _ in range(4)]
            bev = [Buf() for _ in range(4)]
            cnt = [0]

            def consume(ct, tb, ps, bps):
                k = cnt[0] % 4
                cnt[0] += 1
                if (8 <= ct < 16) or (44 <= ct < 68) or (84 <= ct < 92):
                    K.op("act", (lambda e, k=k, ps=ps: e.activation(out=ev[k][:], in_=ps[:, :], func=AF.Silu)), reads=[bps], writes=[bev[k]])
                elif ct == 92:
                    K.op("act", (lambda e, k=k, ps=ps: e.activation(out=ev[k][:], in_=ps[:, :], func=AF.Sigmoid)), reads=[bps], writes=[bev[k]])
                else:
                    K.op("act", (lambda e, k=k, ps=ps: e.copy(out=ev[k][:], in_=ps[:, :])), reads=[bps], writes=[bev[k]])
                K.dma("act", self.projT[_cs(ct), tb * 512:(tb + 1) * 512], ev[k][:], reads=[bev[k]])

            self._proj_loop(big, bbig, self.win, l * NCT_IN, NCT_IN, consume)
            self.end_phase()

    def phase2_rghg(self, l):
        K = self.K
        with ExitStack() as es:
            K.les = es
            _interleave(K, [lambda: self.phase2_rg(l, (0, 2, 4, 6)), lambda: self.phase2_rg(l, (1, 3, 5, 7))])
            self.end_phase()
        with ExitStack() as es:
            K.les = es
            RM = K.sb("hgRM", [128, T])
            bRM = Buf()
            K.op("pool", lambda e: e.memset(RM[:], 1.0), writes=[bRM])
            K.op("pool", lambda e: e.memset(RM[:, :].rearrange("p (c t) -> p c t", t=128)[:, :, 0:1], 0.0), writes=[bRM])
            _interleave(K, [lambda: self.phase2_hg(l, range(0, 4), RM, bRM), lambda: self.phase2_hg(l, range(4, 8), RM, bRM)])
            self.end_phase()

    def phase2_rg(self, l, ns):
        K = self.K
        P = self.prm_sb
        if True:
            wg = K.sb("rgw", [128, 2, 8, 128])
            bwg = Buf()
            K.dma("sp", wg[:].rearrange("p a n e -> p (a n e)"), self.rgw[l, :, :], writes=[bwg])
            xp = K.sb("xp", [128, T + 3]); bxp = Buf()
            xc = K.sb("xc", [128, T]); bxc = Buf()
            R = K.sb("R", [128, T]); bR = Buf()
            I_ = K.sb("I", [128, T]); bI = Buf()
            A = K.sb("A", [128, T]); bA = Buf()
            A2 = K.sb("A2", [128, T]); bA2 = Buf()
            H = K.sb("H", [128, T]); bH = Buf()
            G = K.sb("G", [128, T]); bG = Buf()
            Y = K.sb("Y", [128, T], BF16); bY = Buf()
            ps = [K.ps("rgp", [128, 512]) for _ in range(2)]
            bps = [Buf() for _ in range(2)]
            K.op("pool", lambda e: e.memset(xp[:, 0:3], 0.0), writes=[bxp])
            c8 = self.misc[:, l * 16:l * 16 + 8]
            c16 = self.misc[:, l * 16 + 8:l * 16 + 16]
            for n in ns:
                K.dma("sp", xp[:, 3:], self.projT[OFF["rgx"] + n * 128:OFF["rgx"] + (n + 1) * 128, :], writes=[bxp])
                K.dma("act", G[:], self.projT[OFF["rgg"] + n * 128:OFF["rgg"] + (n + 1) * 128, :], writes=[bG])
                K.op("dve", (lambda e, n=n: e.tensor_scalar(
                    out=xc[:], in0=xp[:, 0:T], scalar1=P[:, l, P_CONVW + n * 4:P_CONVW + n * 4 + 1],
                    scalar2=P[:, l, P_CONVB + n:P_CONVB + n + 1], op0=ALU.mult, op1=ALU.add)),
                    reads=[bxp, self.b_prm], writes=[bxc])
                for j in range(1, 4):
                    K.op("dve", (lambda e, n=n, j=j: e.scalar_tensor_tensor(
                        out=xc[:], in0=xp[:, j:j + T], scalar=P[:, l, P_CONVW + n * 4 + j:P_CONVW + n * 4 + j + 1],
                        in1=xc[:], op0=ALU.mult, op1=ALU.add)), reads=[bxp, bxc], writes=[bxc])
                pi = 0
                for gi, (dst, bdst, pb) in enumerate(((R, bR, P_BA), (I_, bI, P_BX))):
                    for tb in range(4):
                        p = pi % 2
                        pi += 1
                        K.op("pe", (lambda e, gi=gi, n=n, tb=tb, p=p: e.matmul(
                            ps[p][:, :], lhsT=wg[:, gi, n, :], rhs=xc[:, tb * 512:(tb + 1) * 512], start=True, stop=True)),
                            reads=[bwg, bxc], writes=[bps[p]])
                        K.op("act", (lambda e, dst=dst, tb=tb, p=p, pb=pb, n=n: e.activation(
                            out=dst[:, tb * 512:(tb + 1) * 512], in_=ps[p][:, :], func=AF.Sigmoid,
                            bias=P[:, l, pb + n:pb + n + 1])), reads=[bps[p], self.b_prm], writes=[bdst])
                K.op("act", (lambda e, n=n: e.activation(out=A[:], in_=R[:], func=AF.Exp, scale=c8[:, n:n + 1])),
                     reads=[bR, self.b_misc], writes=[bA])
                K.op("act", (lambda e, n=n: e.activation(out=A2[:], in_=R[:], func=AF.Exp, scale=c16[:, n:n + 1])),
                     reads=[bR, self.b_misc], writes=[bA2])
                K.op("dve", (lambda e: e.tensor_scalar_min(out=A2[:], in0=A2[:], scalar1=1.0 - 6e-8)), reads=[bA2], writes=[bA2])
                K.op("act", (lambda e: e.activation(out=A2[:], in_=A2[:], func=AF.Sqrt, scale=-1.0, bias=1.0)),
                     reads=[bA2], writes=[bA2])
                K.op("dve", (lambda e: e.memset(A2[:, 0:1], 1.0)), writes=[bA2])
                K.op("dve", (lambda e: e.tensor_tensor(out=I_[:], in0=I_[:], in1=xc[:], op=ALU.mult)), reads=[bI, bxc], writes=[bI])
                K.op("dve", (lambda e: e.tensor_tensor(out=I_[:], in0=I_[:], in1=A2[:], op=ALU.mult)), reads=[bI, bA2], writes=[bI])
                K.op("dve", (lambda e: e.tensor_tensor_scan(out=H[:], data0=A[:], data1=I_[:], initial=0.0,
                                                            op0=ALU.mult, op1=ALU.add)), reads=[bA, bI], writes=[bH])
                K.op("dve", (lambda e: e.tensor_tensor(out=Y[:], in0=H[:], in1=G[:], op=ALU.mult)), reads=[bH, bG], writes=[bY])
                K.dma("sp", self.yT[_cs(n), :], Y[:], reads=[bY])

    def phase2_hg(self, l, heads, RM, bRM):
        K = self.K
        P = self.prm_sb
        if True:
            names = ["Z", "Q", "G", "Kb", "B", "E", "QT", "KT", "O"]
            tl = {n: K.sb("hg" + n, [128, T]) for n in names}
            bf = {n: Buf(n) for n in names}
            Z, Q, G, Kb, B, E, QT, KT, O = (tl[n] for n in names)
            tl["VT"] = E
            bf["VT"] = bf["E"]
            VT = E
            bf["RM"] = bRM
            VTM = K.sb("VTM", [128, 16, 128]); bVTM = Buf()
            Yb = K.sb("hgY", [128, T], BF16); bYb = Buf()
            cm = K.sb("cm", [128, 128], I32); bcm = Buf()
            K.dma("sp", cm[:], self.cmask[:, :], writes=[bcm])
            S = [K.sb("S", [128, 128]) for _ in range(2)]; bS = [Buf() for _ in range(2)]
            ATs = [K.sb("ATs", [128, 128]) for _ in range(2)]; bATs = [Buf() for _ in range(2)]
            KDT = [K.sb("KDT", [128, 128]) for _ in range(2)]; bKDT = [Buf() for _ in range(2)]
            EBL = K.sb("EBL", [128, 16]); bEBL = Buf()
            _pa = K.ps("hpa", [128, 512]); _bpa = Buf()
            ps_a = [_pa, _pa]; bpa = [_bpa, _bpa]
            _po = K.ps("hpo", [128, 512]); _bpo = Buf()
            ps_o = [_po, _po]; bpo = [_bpo, _bpo]
            _pts = K.ps("hpts", [128, 512])
            ps_t = _pts[:, 0:256]; bpt = Buf()
            ps_s = _pts[:, 256:512]; bpss = Buf()
            _pm = K.ps("hpm", [128, 512]); _bpm = Buf()
            ps_m = [_pm, _pm]; bpm = [_bpm, _bpm]
            oml = self.misc[:, 32 + l * 8:32 + l * 8 + 8]
            B3 = B[:, :].rearrange("p (c t) -> p c t", t=128)
            E3 = E[:, :].rearrange("p (c t) -> p c t", t=128)
            for h in heads:
                for nm, off, q in (("Z", "hf", "sp"), ("Q", "hq", "act"), ("VT", "hi", "sp"), ("G", "hg", "act")):
                    K.dma(q, tl[nm][:], self.projT[OFF[off] + h * 128:OFF[off] + (h + 1) * 128, :], writes=[bf[nm]])
                for c4 in range(4):
                    p = c4 % 2
                    for j in range(4):
                        c = c4 * 4 + j
                        K.op("pe", (lambda e, p=p, j=j, c=c: e.transpose(out=ps_m[p][:, _cs(j)], in_=VT[:, _cs(c)], identity=self.ident)),
                             reads=[bf["VT"], self.b_cf], writes=[bpm[p]])
                    K.op("act", (lambda e, p=p, c4=c4: e.copy(out=VTM[:, c4 * 4:(c4 + 1) * 4, :].rearrange("p c d -> p (c d)"), in_=ps_m[p][:, :])),
                         reads=[bpm[p]], writes=[bVTM])
                K.op("act", (lambda e: e.activation(out=Kb[:], in_=Z[:], func=AF.Sigmoid, scale=-1.0)), reads=[bf["Z"]], writes=[bf["Kb"]])
                K.op("dve", (lambda e, h=h: e.tensor_scalar(out=Kb[:], in0=Kb[:], scalar1=oml[:, h:h + 1], scalar2=None, op0=ALU.mult)),
                     reads=[bf["Kb"], self.b_misc], writes=[bf["Kb"]])
                K.op("dve", (lambda e: e.tensor_scalar(out=Z[:], in0=Kb[:], scalar1=-1.0, scalar2=1.0, op0=ALU.mult, op1=ALU.add)),
                     reads=[bf["Kb"]], writes=[bf["Z"]])
                K.op("act", (lambda e: e.activation(out=Z[:], in_=Z[:], func=AF.Ln)), reads=[bf["Z"]], writes=[bf["Z"]])
                K.op("dve", (lambda e: e.tensor_tensor_scan(out=B[:], data0=RM[:], data1=Z[:], initial=0.0, op0=ALU.mult, op1=ALU.add)),
                     reads=[bf["RM"], bf["Z"]], writes=[bf["B"]])
                K.op("dve", (lambda e: e.tensor_tensor(out=E3, in0=B3, in1=B3[:, :, 63:64].to_broadcast([128, 16, 128]), op=ALU.subtract)),
                     reads=[bf["B"]], writes=[bf["E"]])
                K.op("act", (lambda e: e.activation(out=QT[:], in_=E[:], func=AF.Exp)), reads=[bf["E"]], writes=[bf["QT"]])
                K.op("dve", (lambda e: e.tensor_tensor(out=QT[:], in0=QT[:], in1=Q[:], op=ALU.mult)), reads=[bf["QT"], bf["Q"]], writes=[bf["QT"]])
                K.op("act", (lambda e: e.activation(out=KT[:], in_=E[:], func=AF.Exp, scale=-1.0)), reads=[bf["E"]], writes=[bf["KT"]])
                K.op("dve", (lambda e: e.tensor_tensor(out=KT[:], in0=KT[:], in1=Kb[:], op=ALU.mult)), reads=[bf["KT"], bf["Kb"]], writes=[bf["KT"]])
                K.op("act", (lambda e: e.activation(out=E[:], in_=B[:], func=AF.Exp)), reads=[bf["B"]], writes=[bf["E"]])
                K.op("dve", (lambda e: e.tensor_tensor(out=Q[:], in0=Q[:], in1=E[:], op=ALU.mult)), reads=[bf["Q"], bf["E"]], writes=[bf["Q"]])
                K.op("act", (lambda e: e.activation(out=EBL[:], in_=B3[:, :, 127], func=AF.Exp)), reads=[bf["B"]], writes=[bEBL])
                K.op("dve", (lambda e: e.tensor_tensor(out=E3, in0=B3, in1=B3[:, :, 127:128].to_broadcast([128, 16, 128]), op=ALU.subtract)),
                     reads=[bf["B"]], writes=[bf["E"]])
                K.op("act", (lambda e: e.activation(out=E[:], in_=E[:], func=AF.Exp, scale=-1.0)), reads=[bf["E"]], writes=[bf["E"]])
                K.op("dve", (lambda e: e.tensor_tensor(out=Kb[:], in0=Kb[:], in1=E[:], op=ALU.mult)), reads=[bf["Kb"], bf["E"]], writes=[bf["Kb"]])
                for c in range(16):
                    a = c % 2
                    K.op("pe", (lambda e, a=a, c=c: e.matmul(ps_a[a][:, 0:128], lhsT=KT[:, _cs(c)], rhs=QT[:, _cs(c)], start=True, stop=True)),
                         reads=[bf["KT"], bf["QT"]], writes=[bpa[a]])
                    K.op("pool", (lambda e, a=a: e.memset(ATs[a][:], 0.0)), writes=[bATs[a]])
                    K.op("dve", (lambda e, a=a: e.copy_predicated(out=ATs[a][:], mask=cm[:], data=ps_a[a][:, 0:128])),
                         reads=[bpa[a], bcm], writes=[bATs[a]])
                    K.op("pe", (lambda e, a=a, c=c: e.matmul(ps_o[a][:, 0:128], lhsT=VTM[:, c, :], rhs=ATs[a][:], start=True, stop=(c == 0))),
                         reads=[bVTM, bATs[a]], writes=[bpo[a]])
                    if c > 0:
                        K.op("pe", (lambda e, a=a, c=c: e.matmul(ps_o[a][:, 0:128], lhsT=S[(c - 1) % 2][:], rhs=Q[:, _cs(c)], start=False, stop=True)),
                             reads=[bS[(c - 1) % 2], bf["Q"]], writes=[bpo[a]])
                    K.op("act", (lambda e, a=a, c=c: e.copy(out=O[:, _cs(c)], in_=ps_o[a][:, 0:128])), reads=[bpo[a]], writes=[bf["O"]])
                    if c < 15:
                        K.op("pe", (lambda e, c=c: e.transpose(out=ps_t[:, 0:128], in_=Kb[:, _cs(c)], identity=self.ident)),
                             reads=[bf["Kb"], self.b_cf], writes=[bpt])
                        K.op("act", (lambda e, a=a: e.copy(out=KDT[a][:], in_=ps_t[:, 0:128])), reads=[bpt], writes=[bKDT[a]])
                        K.op("pe", (lambda e, a=a, c=c: e.matmul(ps_s[:, 0:128], lhsT=KDT[a][:], rhs=VTM[:, c, :], start=True, stop=True)),
                             reads=[bKDT[a], bVTM], writes=[bpss])
                        if c == 0:
                            K.op("dve", (lambda e: e.tensor_copy(out=S[0][:], in_=ps_s[:, 0:128])), reads=[bpss], writes=[bS[0]])
                        else:
                            K.op("dve", (lambda e, c=c: e.scalar_tensor_tensor(
                                out=S[c % 2][:], in0=S[(c - 1) % 2][:], scalar=EBL[:, c:c + 1], in1=ps_s[:, 0:128],
                                op0=ALU.mult, op1=ALU.add)), reads=[bS[(c - 1) % 2], bEBL, bpss], writes=[bS[c % 2]])
                K.op("act", (lambda e: e.activation(out=E[:], in_=O[:], func=AF.Square)), reads=[bf["O"]], writes=[bf["E"]])
                for tb in range(4):
                    p = tb % 2
                    K.op("pe", (lambda e, p=p, tb=tb: e.matmul(ps_m[p][:, :], lhsT=self.ones, rhs=E[:, tb * 512:(tb + 1) * 512], start=True, stop=True)),
                         reads=[self.b_cf, bf["E"]], writes=[bpm[p]])
                    K.op("dve", (lambda e, p=p, tb=tb: e.tensor_scalar(out=QT[:, tb * 512:(tb + 1) * 512], in0=ps_m[p][:, :], scalar1=1.0 / 128,
                                                                   scalar2=RMS_EPS, op0=ALU.mult, op1=ALU.add)), reads=[bpm[p]], writes=[bf["QT"]])
                K.op("act", (lambda e: e.activation(out=QT[:], in_=QT[:], func=AF.Sqrt)), reads=[bf["QT"]], writes=[bf["QT"]])
                K.op("dve", (lambda e: e.reciprocal(out=QT[:], in_=QT[:])), reads=[bf["QT"]], writes=[bf["QT"]])
                K.op("dve", (lambda e: e.tensor_tensor(out=O[:], in0=O[:], in1=QT[:], op=ALU.mult)), reads=[bf["O"], bf["QT"]], writes=[bf["O"]])
                K.op("dve", (lambda e: e.scalar_tensor_tensor(out=Yb[:], in0=O[:], scalar=P[:, l, P_NG:P_NG + 1], in1=G[:],
                                                              op0=ALU.mult, op1=ALU.mult)), reads=[bf["O"], bf["G"], self.b_prm], writes=[bYb])
                K.dma("sp", self.yT[_cs(24 + h), :], Yb[:], reads=[bYb])

    def phase2_nsa(self, l):
        K = self.K
        P = self.prm_sb
        with ExitStack() as es:
            K.les = es
            cb = K.sb("cb", [128, 7168], BF16); bcb = Buf()
            K.dma("sp", cb[:], self.cb[:, :], writes=[bcb])
            LD = cb[:, 0:4096].rearrange("p (d j) -> p d j", j=128)
            EX = cb[:, 5120:7168].rearrange("p (k j) -> p k j", j=128)
            BT = K.sb("BT", [128, 6, 512]); bBT = Buf()
            qTb = K.sb("qTb", [128, 8, T], BF16); bq = Buf()
            kTs = K.sb("kTs", [128, T], BF16); bkTs = Buf()
            kTw = K.sb("kTw", [128, T], BF16); bkTw = Buf()
            V1s = K.sb("V1s", [128, 16, 129], BF16); bV1s = Buf()
            V1w = K.sb("V1w", [128, 16, 129], BF16); bV1w = Buf()
            KC = K.sb("KC", [128, T]); bKC = Buf()
            stg = [K.sb("stg", [128, T]) for _ in range(2)]; bstg = [Buf() for _ in range(2)]
            W1 = K.sb("W1", [128, 32, 128]); bW1 = Buf()
            W2 = K.sb("W2", [128, 2, 2, 128]); bW2 = Buf()
            K.dma("sp", W2[:].rearrange("p a h d -> p (a h d)"), self.cw2[l, :, :], writes=[bW2])
            HS = K.sb("HS", [128, 2, 128]); bHS = Buf()
            CST = K.sb("CST", [128, 2]); bCST = Buf()
            KCMP = K.sb("KCMP", [128, 128], BF16); bKCMP = Buf()
            VC1 = K.sb("VC1", [128, 161]); bVC1 = Buf()
            GLT = K.sb("GLT", [128, 32, 128]); bGLT = Buf()
            K.op("pool", lambda e: e.memset(GLT[:], 0.0), writes=[bGLT])
            SG = [K.sb("SG", [128, 4, 3]) for _ in range(2)]; bSG = [Buf() for _ in range(2)]
            FOLD = K.sb("FOLD", [128, 128]); bFOLD = Buf()
            K.op("pool", lambda e: e.memset(FOLD[:], 0.0), writes=[bFOLD])
            K.dma("sp", FOLD[:, 0:64], self.cf[:, 288:352], writes=[bFOLD])
            IMP2 = K.sb("IMP2", [128, 32]); bIMP2 = Buf()
            SGATE = [K.sb("SGATE", [128, 8, 256]) for _ in range(2)]; bSGATE = [Buf() for _ in range(2)]
            s_sb = [K.sb("s_sb", [128, 512]) for _ in range(4)]; bs_sb = [Buf() for _ in range(4)]
            s_sb2 = K.sb("s_sb2", [128, 512]); bs_sb2 = Buf()
            ecf = K.sb("ecf", [128, 512]); becf = Buf()
            eb = [K.sb("eb", [128, 512], BF16) for _ in range(4)]; beb = [Buf() for _ in range(4)]
            ACCs = [K.sb("ACC", [128, 4, 128]) for _ in range(2)]; bACCs = [Buf() for _ in range(2)]
            RD = K.sb("RD", [128, 4]); bRD = Buf()
            WG = K.sb("WG", [128, 4]); bWG = Buf()
            UT = K.sb("UT", [128, 4, 32]); bUT = Buf()
            IMP = K.sb("IMP", [64, 32]); bIMP = Buf()
            TMP = K.sb("TMP", [64, 32]); bTMP = Buf()
            M8 = K.sb("M8", [64, 16]); bM8 = Buf()
            MK = K.sb("MK", [128, 128]); bMK = Buf()
            K.op("pool", lambda e: e.memset(MK[:], 0.0), writes=[bMK])
            MBT = K.sb("MBT", [128, 8, 64], BF16); bMBT = Buf()
            K.op("pool", lambda e: e.memset(MBT[:], 0.0), writes=[bMBT])
            Yt = [K.sb("Yt", [128, 8, 64], BF16) for _ in range(2)]; bYt = [Buf() for _ in range(2)]
            OS = [K.sb("OS", [128, 2, 512]) for _ in range(2)]; bOS = [Buf() for _ in range(2)]
            osi = [0]
            ps_s = [K.ps("nps", [128, 512]) for _ in range(4)]; bps_s = [Buf() for _ in range(4)]
            psO = [K.ps("npo", [128, 512]) for _ in range(2)]; bpsO = Buf()
            ps_y = K.ps("npy", [128, 512]); bps_y = Buf()
            _pm = K.ps("npm", [128, 512]); _bpm = Buf()
            ps_m = [_pm, _pm]; bps_m = [_bpm, _bpm]
            K.op("pool", lambda e: e.memset(V1s[:, :, 128:129], 1.0), writes=[bV1s])
            K.op("pool", lambda e: e.memset(V1w[:, :, 128:129], 1.0), writes=[bV1w])
            K.op("pool", lambda e: e.memset(KCMP[:], 0.0), writes=[bKCMP])
            K.op("pool", lambda e: e.memset(VC1[:], 0.0), writes=[bVC1])
            K.op("pool", lambda e: e.memset(VC1[:, 128:129], 1.0), writes=[bVC1])
            K.dma("sp", VC1[0:127, 129:161], self.cf[0:127, 256:288], writes=[bVC1])
            sti = [0]
            freg = []

            def fillreg(e):
                if not freg:
                    freg.append(e.to_reg(NEG))
                return freg[0]

            def load_stage(row0):
                i = sti[0] % 2
                sti[0] += 1
                K.dma("sp" if i else "act", stg[i][:], self.projT[row0:row0 + 128, :], writes=[bstg[i]])
                return stg[i], bstg[i]

            def opsO(r):
                return psO[r // 3][0:64, (r % 3) * 161:(r % 3) * 161 + 161]

            for g in range(2):
                SC = cb[:, 4096 + g * 512:4096 + (g + 1) * 512]
                K.dma("sp", BT[:].rearrange("p a n -> p (a n)"), self.cbt[g, :, :], writes=[bBT])
                glsrc = self.projT[OFF["gl"] + g * 24:OFF["gl"] + (g + 1) * 24, :].rearrange("p (q t) -> p q t", t=64)
                K.dma("sp", GLT[0:24, :, 0:64], glsrc, writes=[bGLT])
                K.dma("act", GLT[0:24, :, 64:128], glsrc, writes=[bGLT])
                for r in range(8):
                    st, bst = load_stage(OFF["q"] + (g * 8 + r) * 128)
                    K.op("dve" if r % 2 else "act", (lambda e, st=st, r=r: (e.tensor_copy(out=qTb[:, r, :], in_=st[:]) if r % 2 else e.copy(out=qTb[:, r, :], in_=st[:]))),
                         reads=[bst], writes=[bq])
                for dst, bdst, off in ((kTs, bkTs, OFF["kvs"]), (kTw, bkTw, OFF["kvw"])):
                    st, bst = load_stage(off + g * 128)
                    K.op("dve", (lambda e, st=st, dst=dst: e.tensor_copy(out=dst[:], in_=st[:])), reads=[bst], writes=[bdst])
                for dst, bdst, off in ((V1s, bV1s, OFF["kvs"]), (V1w, bV1w, OFF["kvw"])):
                    st, bst = load_stage(off + 256 + g * 128)
                    for c4 in range(4):
                        p = c4 % 2
                        for j in range(4):
                            c = c4 * 4 + j
                            K.op("pe", (lambda e, p=p, j=j, c=c, st=st: e.transpose(out=ps_m[p][:, _cs(j)], in_=st[:, _cs(c)], identity=self.ident)),
                                 reads=[bst, self.b_cf], writes=[bps_m[p]])
                        K.op("act", (lambda e, p=p, c4=c4, dst=dst: e.copy(out=dst[:, c4 * 4:(c4 + 1) * 4, 0:128],
                                                                       in_=ps_m[p][:, :].rearrange("p (c d) -> p c d", d=128))),
                             reads=[bps_m[p]], writes=[bdst])
                for xi, (off, pcol) in enumerate(((OFF["kvc"] + g * 128, P_PEK), (OFF["kvc"] + 256 + g * 128, P_PEV))):
                    K.dma("sp", KC[:], self.projT[off:off + 128, :], writes=[bKC])
                    for hc in range(2):
                        K.dma("act", W1[:], self.cw1[l, xi, :, :].rearrange("p (l h) -> p l h", h=256)[:, :, _cs(hc)], writes=[bW1])
                        for li in range(32):
                            K.op("pe", (lambda e, hc=hc, li=li, pcol=pcol: e.matmul(
                                ps_m[0][:, 256 + 2 * hc:256 + 2 * hc + 1], lhsT=W1[:, li, :], rhs=P[:, l, pcol + li:pcol + li + 1],
                                start=(li == 0), stop=(li == 31))), reads=[bW1, self.b_prm], writes=[bps_m[0]])
                        K.op("dve", (lambda e, hc=hc: e.tensor_copy(out=CST[:, hc:hc + 1], in_=ps_m[0][:, 256 + 2 * hc:256 + 2 * hc + 1])),
                             reads=[bps_m[0]], writes=[bCST])
                        for li in range(32):
                            K.op("pe", (lambda e, hc=hc, li=li: e.matmul(
                                ps_m[1][:, hc * 128:hc * 128 + 127], lhsT=W1[:, li, :], rhs=KC[:, li:li + 2017:16],
                                start=(li == 0), stop=(li == 31))), reads=[bW1, bKC], writes=[bps_m[1]])
                        K.op("act", (lambda e, hc=hc: e.activation(out=HS[:, hc, 0:127], in_=ps_m[1][:, hc * 128:hc * 128 + 127],
                                                                   func=AF.Silu, bias=CST[:, hc:hc + 1])), reads=[bps_m[1], bCST], writes=[bHS])
                    if xi == 0:
                        for hc in range(2):
                            K.op("pe", (lambda e, hc=hc: e.matmul(ps_m[0][:, 0:127], lhsT=W2[:, 0, hc, :], rhs=HS[:, hc, 0:127],
                                                                  start=(hc == 0), stop=(hc == 1))), reads=[bW2, bHS], writes=[bps_m[0]])
                        K.op("act", (lambda e: e.copy(out=KCMP[:, 0:127], in_=ps_m[0][:, 0:127])), reads=[bps_m[0]], writes=[bKCMP])
                    else:
                        for hc in range(2):
                            K.op("pe", (lambda e, hc=hc: e.matmul(ps_m[0][0:127, 0:128], lhsT=HS[:, hc, 0:127], rhs=W2[:, 1, hc, :],
                                                                  start=(hc == 0), stop=(hc == 1))), reads=[bW2, bHS], writes=[bps_m[0]])
                        K.op("act", (lambda e: e.copy(out=VC1[0:127, 0:128], in_=ps_m[0][0:127, 0:128])), reads=[bps_m[0]], writes=[bVC1])
                pair = [0]

                def score_pair(qb, lhsT_main, breads, delta, bias_idx, mask_kt, nrows=128, SC=SC):
                    i = pair[0] % 4
                    j = pair[0] % 4
                    pair[0] += 1
                    rq = qTb[:, :, qb * 64:(qb + 1) * 64]
                    last = "main" if (delta == 0 and mask_kt is None) else ("mask" if mask_kt is not None else "delta")
                    K.op("pe", (lambda e, i=i, rq=rq: e.matmul(ps_s[i][0:nrows, :], lhsT=lhsT_main, rhs=rq, start=True, stop=(last == "main"))),
                         reads=list(breads) + [bq], writes=[bps_s[i]])
                    if delta > 0:
                        K.op("pe", (lambda e, i=i: e.matmul(ps_s[i][0:nrows, :], lhsT=LD[:, delta, 0:nrows], rhs=SC, start=False, stop=(last == "delta"))),
                             reads=[bcb], writes=[bps_s[i]])
                    if mask_kt is not None:
                        K.op("pe", (lambda e, i=i: e.matmul(ps_s[i][0:nrows, :], lhsT=EX[:, mask_kt, :], rhs=MBT[:].rearrange("p r t -> p (r t)"),
                                                            start=False, stop=True)), reads=[bcb, bMBT], writes=[bps_s[i]])
                    K.op("dve", (lambda e, i=i, j=j: e.scalar_tensor_tensor(out=s_sb[j][0:nrows, :], in0=ps_s[i][0:nrows, :], scalar=SCALE,
                                                                            in1=BT[0:nrows, bias_idx, :], op0=ALU.mult, op1=ALU.add)),
                         reads=[bps_s[i], bBT], writes=[bs_sb[j]])
                    return j

                def pv(j_e, etile, betile, vrhs, bv, first, nrows=128, ncols=129):
                    for m in range(4):
                        K.op("pe", (lambda e, m=m: e.matmul(
                            psO[m // 3][:, (m % 3) * 161:(m % 3) * 161 + ncols], lhsT=etile[0:nrows, m * 128:(m + 1) * 128], rhs=vrhs,
                            start=(first and m % 3 == 0), stop=True, skip_group_check=True)), reads=[betile, bv], writes=[bpsO])

                def combine(branch, first, qb):
                    oi = osi[0] % 2
                    osi[0] += 1
                    ACC = ACCs[qb % 2]
                    bACC = bACCs[qb % 2]
                    O_ = OS[oi]
                    bO = bOS[oi]
                    K.op("act", (lambda e: e.copy(out=O_[:, 0, 0:483], in_=psO[0][:, 0:483])), reads=[bpsO], writes=[bO])
                    K.op("dve", (lambda e: e.tensor_copy(out=O_[:, 1, 0:161], in_=psO[1][:, 0:161])), reads=[bpsO], writes=[bO])
                    K.op("dve", (lambda e: e.tensor_scalar_add(out=RD[:, 0:3], in0=O_[:, 0, 0:483].rearrange("p (h c) -> p h c", c=161)[:, :, 128], scalar1=1e-30)),
                         reads=[bO], writes=[bRD])
                    K.op("dve", (lambda e: e.tensor_scalar_add(out=RD[:, 3:4], in0=O_[:, 1, 128:129], scalar1=1e-30)), reads=[bO], writes=[bRD])
                    K.op("dve", (lambda e: e.reciprocal(out=RD[:], in_=RD[:])), reads=[bRD], writes=[bRD])
                    K.op("dve", (lambda e: e.tensor_tensor(out=WG[:], in0=RD[:], in1=SG[qb % 2][:, :, branch], op=ALU.mult)),
                         reads=[bRD, bSG[qb % 2]], writes=[bWG])
                    for m in range(4):
                        o = O_[:, m // 3, (m % 3) * 161:(m % 3) * 161 + 128]
                        if first:
                            K.op("dve", (lambda e, m=m, o=o: e.tensor_scalar(out=ACC[:, m, :], in0=o, scalar1=WG[:, m:m + 1], scalar2=None, op0=ALU.mult)),
                                 reads=[bO, bWG], writes=[bACC])
                        else:
                            K.op("dve", (lambda e, m=m, o=o: e.scalar_tensor_tensor(out=ACC[:, m, :], in0=o, scalar=WG[:, m:m + 1], in1=ACC[:, m, :],
                                                                                    op0=ALU.mult, op1=ALU.add)), reads=[bO, bWG, bACC], writes=[bACC])
                    return O_, bO

                pairs = []
                deferred = []

                def mk_pre(qb):
                    def pre():
                        if qb % 4 == 0:
                            sgi = (qb // 4) % 2
                            for r in range(8):
                                r0 = OFF["nsag"] + (g * 8 + r) * 128
                                K.dma("sp" if r % 2 else "act", SGATE[sgi][:, r, :], self.projT[r0:r0 + 128, qb * 64:qb * 64 + 256], writes=[bSGATE[sgi]])
                        K.op("pe", (lambda e: e.transpose(out=ps_m[0][:, 0:128], in_=GLT[:, qb, :], identity=self.ident)),
                             reads=[bGLT, self.b_cf], writes=[bps_m[0]])
                        for r2 in range(2):
                            K.op("act", (lambda e, r2=r2: e.copy(
                                out=SG[qb % 2][r2 * 64:(r2 + 1) * 64, :, :],
                                in_=ps_m[0][r2 * 64:(r2 + 1) * 64, 0:24].rearrange("p (m r b) -> p m r b", r=2, b=3)[:, :, r2, :])),
                                reads=[bps_m[0]], writes=[bSG[qb % 2]])
                    return pre

                def mk_cmp(qb):
                    def score():
                        return score_pair(qb, KCMP[:, 0:127], [bKCMP], qb, 5, None, nrows=127)

                    def fin(j):
                        for r in range(8):
                            K.op("pool", (lambda e, r=r: e.affine_select(
                                out=s_sb2[0:127, r * 64:(r + 1) * 64], in_=s_sb[j][0:127, r * 64:(r + 1) * 64], pattern=[[1, 64]],
                                compare_op=ALU.is_ge, fill=fillreg(e), base=64 * qb - 31, channel_multiplier=-16)), reads=[bs_sb[j]], writes=[bs_sb2])
                        K.op("act", (lambda e: e.activation(out=ecf[0:127, :], in_=s_sb2[0:127, :], func=AF.Exp)), reads=[bs_sb2], writes=[becf])
                        pv(None, ecf, becf, VC1[0:127, :], bVC1, True, nrows=127, ncols=161)

                    def post():
                        O_, bO = combine(0, True, qb)
                        if qb >= 16:
                            for m in range(4):
                                u = O_[:, m // 3, (m % 3) * 161 + 129:(m % 3) * 161 + 161]
                                K.op("dve", (lambda e, m=m, u=u: e.tensor_scalar(out=UT[:, m, :], in0=u, scalar1=RD[:, m:m + 1], scalar2=None, op0=ALU.mult)),
                                     reads=[bO, bRD], writes=[bUT])
                            K.op("dve", (lambda e: e.tensor_reduce(out=IMP2[:], in_=UT[:].rearrange("p r n -> p n r"), axis=AX.X, op=ALU.add)),
                                 reads=[bUT], writes=[bIMP2])
                            K.op("pe", (lambda e: e.matmul(ps_m[1][:, 128:160], lhsT=FOLD[:], rhs=IMP2[:], start=True, stop=True)),
                                 reads=[bFOLD, bIMP2], writes=[bps_m[1]])
                            K.op("dve", (lambda e: e.tensor_copy(out=IMP[:], in_=ps_m[1][0:64, 128:160])), reads=[bps_m[1]], writes=[bIMP])
                            if qb < 31:
                                K.op("dve", (lambda e: e.memset(IMP[:, qb + 1:32], -1e30)), writes=[bIMP])
                            for col in (0, qb, qb - 1):
                                K.op("dve", (lambda e, col=col: e.memset(IMP[:, col:col + 1], 1e9)), writes=[bIMP])
                            K.op("dve", (lambda e: e.max(out=M8[:, 0:8], in_=IMP[:])), reads=[bIMP], writes=[bM8])
                            K.op("dve", (lambda e: e.match_replace(out=TMP[:], in_to_replace=M8[:, 0:8], in_values=IMP[:], imm_value=-3e38)),
                                 reads=[bIMP, bM8], writes=[bTMP])
                            K.op("dve", (lambda e: e.max(out=M8[:, 8:16], in_=TMP[:])), reads=[bTMP], writes=[bM8])
                            K.op("dve", (lambda e: e.tensor_scalar(out=MK[0:64, 0:32], in0=IMP[:], scalar1=M8[:, 15:16], scalar2=None, op0=ALU.is_ge)),
                                 reads=[bIMP, bM8], writes=[bMK])
                            K.op("dve", (lambda e: e.tensor_scalar(out=MK[0:64, 0:32], in0=MK[0:64, 0:32], scalar1=-1.0, scalar2=BIGS, op0=ALU.add, op1=ALU.mult)),
                                 reads=[bMK], writes=[bMK])
                            def later():
                                K.op("pe", (lambda e: e.transpose(out=ps_m[1][:, 0:128], in_=MK[:], identity=self.ident)),
                                     reads=[bMK, self.b_cf], writes=[bps_m[1]])
                                K.op("dve", (lambda e: e.tensor_copy(out=MBT[0:32, :, :], in_=ps_m[1][0:32, 0:64].unsqueeze(1).to_broadcast([32, 8, 64]))),
                                     reads=[bps_m[1]], writes=[bMBT])
                            deferred.append([2, later])
                    return dict(pre=mk_pre(qb), score=score, fin=fin, post=post, barrier=False, is_cmp=True)

                def mk_kv(qb, kt, delta, bidx, kT, bkT, V1, bV1, first, masked, post, barrier):
                    def score():
                        return score_pair(qb, kT[:, _cs(kt)], [bkT], delta, bidx, kt if masked else None)

                    def fin(j):
                        K.op("act", (lambda e: e.activation(out=eb[j][:], in_=s_sb[j][:], func=AF.Exp)), reads=[bs_sb[j]], writes=[beb[j]])
                        pv(j, eb[j], beb[j], V1[:, kt, :], bV1, first)
                    return dict(pre=None, score=score, fin=fin, post=post, barrier=barrier)

                def mk_final(qb):
                    def post():
                        combine(1, False, qb)
                        deferred.append([2, later])

                    def later():
                        ACC = ACCs[qb % 2]
                        bACC = bACCs[qb % 2]
                        for m in range(4):
                            K.op("pe", (lambda e, m=m: e.transpose(out=ps_y[:, m * 128:(m + 1) * 128], in_=ACC[:, m, :], identity=self.ident)),
                                 reads=[bACC, self.b_cf], writes=[bps_y])
                        sgi = (qb // 4) % 2
                        yi = qb % 2
                        K.op("dve", (lambda e: e.tensor_tensor(
                            out=Yt[yi][:], in0=ps_y[:, :].rearrange("p (r t) -> p r t", t=64),
                            in1=SGATE[sgi][:, :, (qb % 4) * 64:(qb % 4 + 1) * 64], op=ALU.mult)), reads=[bps_y, bSGATE[sgi]], writes=[bYt[yi]])
                        r0 = (8 + g * 8) * 128
                        K.dma("sp", self.yT[r0:r0 + 1024, qb * 64:(qb + 1) * 64].rearrange("(r p) t -> p r t", p=128), Yt[yi][:], reads=[bYt[yi]])
                    return post

                for qb in range(32):
                    pairs.append(mk_cmp(qb))
                    deltas = [d for d in range(qb % 2, 10, 2) if (qb - d) >= 0]
                    rd = list(reversed(deltas))
                    for n_, delta in enumerate(rd):
                        kt = (qb - delta) // 2
                        bidx = {0: 1, 1: 2, 8: 3, 9: 4}.get(delta, 0)
                        post = (lambda qb=qb: combine(2, False, qb)) if n_ == len(rd) - 1 else None
                        pairs.append(mk_kv(qb, kt, delta, bidx, kTw, bkTw, V1w, bV1w, n_ == 0, False, post, barrier=False))
                    nkt = qb // 2 + 1
                    for kt in range(nkt):
                        delta = qb - 2 * kt
                        bidx = 1 if delta == 0 else (2 if delta == 1 else 0)
                        post = mk_final(qb) if kt == nkt - 1 else None
                        pairs.append(mk_kv(qb, kt, delta, bidx, kTs, bkTs, V1s, bV1s, kt == 0, qb >= 16, post, barrier=(kt == 0 and qb >= 16)))

                pend = []

                def run_deferred(force=False):
                    keep = []
                    for item in deferred:
                        item[0] -= 1
                        if force or item[0] <= 0:
                            item[1]()
                        else:
                            keep.append(item)
                    deferred[:] = keep

                def finish(pd):
                    p_, j_ = pd
                    p_["fin"](j_)
                    if p_["post"] is not None:
                        p_["post"]()

                for p_ in pairs:
                    if p_["barrier"]:
                        while pend and any(q_[0]["post"] is not None and q_[0].get("is_cmp") for q_ in pend):
                            finish(pend.pop(0))
                        run_deferred(force=True)
                    if p_["pre"] is not None:
                        p_["pre"]()
                    j_ = p_["score"]()
                    pend.append((p_, j_))
                    if len(pend) > 3:
                        finish(pend.pop(0))
                    run_deferred()
                while pend:
                    finish(pend.pop(0))
                run_deferred(force=True)
            self.end_phase()

    def phase3(self, l, xsrc, xdst, bxdst):
        K = self.K
        P = self.prm_sb
        with ExitStack() as es:
            K.les = es
            acc1 = K.sb("acc1", [128, T]); bacc1 = Buf()
            acc2 = K.sb("acc2", [128, T]); bacc2 = Buf()
            with ExitStack() as es2:
                K.les = es2
                big = K.sb("big", [128, NKC, T], BF16)
                bbig = Buf()
                for kc in range(NKC):
                    K.dma("sp" if kc % 2 else "act", big[:, kc, :], self.yT[_cs(kc), :], writes=[bbig])
                xr = [K.sb("xr", [128, 512]) for _ in range(3)]; bxr = [Buf() for _ in range(3)]
                vt = [K.sb("vt", [128, 512]) for _ in range(3)]; bvt = [Buf() for _ in range(3)]
                sq = [K.sb("sq", [128, 512]) for _ in range(2)]; bsq = [Buf() for _ in range(2)]
                K.op("pool", lambda e: e.memset(acc1[:], 0.0), writes=[bacc1])
                K.op("pool", lambda e: e.memset(acc2[:], 0.0), writes=[bacc2])
                cnt = [0]

                def consume(ct, tb, ps, bps):
                    k = cnt[0] % 3
                    k2 = cnt[0] % 2
                    cnt[0] += 1
                    ts = slice(tb * 512, (tb + 1) * 512)
                    K.dma("sp", xr[k][:], xsrc[_cs(ct), ts], writes=[bxr[k]])
                    K.op("act", (lambda e, k=k: e.mul(out=xr[k][:], in_=xr[k][:], mul=ALPHA)), reads=[bxr[k]], writes=[bxr[k]])
                    K.op("dve", (lambda e, k=k, ps=ps, ct=ct: e.scalar_tensor_tensor(
                        out=vt[k][:], in0=ps[:, :], scalar=self.modT[:, l, 64 + ct:65 + ct], in1=xr[k][:], op0=ALU.mult, op1=ALU.add)),
                        reads=[bps, bxr[k], self.b_mod], writes=[bvt[k]])
                    K.op("pool", (lambda e, k=k, ts=ts: e.tensor_tensor(out=acc1[:, ts], in0=acc1[:, ts], in1=vt[k][:], op=ALU.add)),
                         reads=[bvt[k], bacc1], writes=[bacc1])
                    K.op("act", (lambda e, k=k, k2=k2: e.activation(out=sq[k2][:], in_=vt[k][:], func=AF.Square)), reads=[bvt[k]], writes=[bsq[k2]])
                    K.op("pool", (lambda e, k2=k2, ts=ts: e.tensor_tensor(out=acc2[:, ts], in0=acc2[:, ts], in1=sq[k2][:], op=ALU.add)),
                         reads=[bsq[k2], bacc2], writes=[bacc2])
                    K.dma("act", self.vT[_cs(ct), ts], vt[k][:], reads=[bvt[k]], writes=[self.b_vT])

                self._proj_loop(big, bbig, self.wout, l * 32, 32, consume, n_wst=2)
                self.end_phase()
            with ExitStack() as es3:
                K.les = es3
                ps = [K.ps("lnp", [128, 512]) for _ in range(4)]; bps = [Buf() for _ in range(4)]
                msq = K.sb("msq", [128, T]); bmsq = Buf()
                for tb in range(4):
                    ts = slice(tb * 512, (tb + 1) * 512)
                    K.op("pe", (lambda e, tb=tb, ts=ts: e.matmul(ps[tb][:, :], lhsT=self.ones, rhs=acc1[:, ts], start=True, stop=True)),
                         reads=[self.b_cf, bacc1], writes=[bps[tb]])
                    K.op("dve", (lambda e, tb=tb, ts=ts: e.tensor_scalar_mul(out=acc1[:, ts], in0=ps[tb][:, :], scalar1=1.0 / D)),
                         reads=[bps[tb]], writes=[bacc1])
                K.op("dve", (lambda e: e.tensor_tensor(out=msq[:], in0=acc1[:], in1=acc1[:], op=ALU.mult)), reads=[bacc1], writes=[bmsq])
                for tb in range(4):
                    ts = slice(tb * 512, (tb + 1) * 512)
                    K.op("pe", (lambda e, tb=tb, ts=ts: e.matmul(ps[tb][:, :], lhsT=self.ones, rhs=acc2[:, ts], start=True, stop=True)),
                         reads=[self.b_cf, bacc2], writes=[bps[tb]])
                    K.op("dve", (lambda e, tb=tb, ts=ts: e.scalar_tensor_tensor(out=acc2[:, ts], in0=ps[tb][:, :], scalar=1.0 / D, in1=msq[:, ts],
                                                                            op0=ALU.mult, op1=ALU.subtract)), reads=[bps[tb], bmsq], writes=[bacc2])
                K.op("act", (lambda e: e.activation(out=acc2[:], in_=acc2[:], func=AF.Sqrt, bias=self.eps_ap)), reads=[bacc2], writes=[bacc2])
                K.op("dve", (lambda e: e.reciprocal(out=acc2[:], in_=acc2[:])), reads=[bacc2], writes=[bacc2])
                vin = [K.sb("vin", [128, T]) for _ in range(2)]; bvin = [Buf() for _ in range(2)]
                xo = [K.sb("xo", [128, T]) for _ in range(2)]; bxo = [Buf() for _ in range(2)]
                for ct in range(32):
                    i = ct % 2
                    K.dma("act", vin[i][:], self.vT[_cs(ct), :], reads=[self.b_vT], writes=[bvin[i]])
                    K.op("dve", (lambda e, i=i: e.tensor_tensor(out=vin[i][:], in0=vin[i][:], in1=acc1[:], op=ALU.subtract)),
                         reads=[bvin[i], bacc1], writes=[bvin[i]])
                    K.op("pool", (lambda e, i=i: e.tensor_tensor(out=vin[i][:], in0=vin[i][:], in1=acc2[:], op=ALU.mult)),
                         reads=[bvin[i], bacc2], writes=[bvin[i]])
                    K.op("act", (lambda e, i=i, ct=ct: e.activation(out=xo[i][:], in_=vin[i][:], func=AF.Identity,
                                                                   scale=P[:, l, P_LNG + ct:P_LNG + ct + 1], bias=P[:, l, P_LNB + ct:P_LNB + ct + 1])),
                         reads=[bvin[i], self.b_prm], writes=[bxo[i]])
                    K.dma("sp", xdst[_cs(ct), :], xo[i][:], reads=[bxo[i]], writes=[bxdst])
                self.end_phase()


def _bf16_split3(x):
    x = np.asarray(x, np.float64)
    h = x.astype(np.float32).astype(ml_dtypes.bfloat16)
    r1 = x - h.astype(np.float64)
    m = r1.astype(np.float32).astype(ml_dtypes.bfloat16)
    r2 = r1 - m.astype(np.float64)
    lo = r2.astype(np.float32).astype(ml_dtypes.bfloat16)
    return h, m, lo


def make_consts():
    cf = np.zeros((128, 352), np.float32)
    for p_ in range(128):
        cf[p_, 288 + p_ % 64] = 1.0
    cf[:, 0:128] = np.eye(128, dtype=np.float32)
    cf[:, 128:256] = 1.0
    n_cmp, n_sel = 127, 32
    s_c = np.arange(n_cmp) * 16
    s_s = np.arange(n_sel) * 64
    ov = np.clip(np.minimum(s_c[:, None] + 32, s_s[None, :] + 64) - np.maximum(s_c[:, None], s_s[None, :]), 0, None).astype(np.float32) / 32
    cf[0:127, 256:288] = ov
    h = np.arange(1, 17, dtype=np.float32)
    slopes = np.power(np.float32(2.0), -8.0 * h / 16).astype(np.float32).reshape(2, 8).astype(np.float64)
    jl = np.arange(128)[:, None, None].astype(np.float64)
    tl = np.arange(64)[None, None, :].astype(np.float64)
    cbt = np.zeros((2, 128, 6, 8, 64), np.float32)
    cb = np.zeros((128, 7168), ml_dtypes.bfloat16)
    for g in range(2):
        s = slopes[g][None, :, None]
        d = tl - jl
        gen = -s * d
        cbt[g, :, 0] = gen
        cbt[g, :, 1] = np.where(d >= 0, gen, NEG)
        cbt[g, :, 2] = np.where(64 + d >= 0, gen, NEG)
        cbt[g, :, 3] = np.where(d < 0, gen, NEG)
        cbt[g, :, 4] = np.where(64 + d < 0, gen, NEG)
        cbt[g, :, 5] = -s * (tl - 16 * jl - 31)
        val = -slopes[g] * 64.0 / SCALE
        parts = _bf16_split3(val)
        sc = np.zeros((24, 8, 64), ml_dtypes.bfloat16)
        for pi, pr in enumerate(parts):
            for r in range(8):
                sc[pi * 8 + r, r, :] = pr[r]
        cb[0:24, 4096 + g * 512:4096 + (g + 1) * 512] = sc.reshape(24, 512)
    ld = np.zeros((24, 32, 128), np.float32)
    ld[:] = np.arange(32, dtype=np.float32)[None, :, None]
    cb[0:24, 0:4096] = ld.reshape(24, 4096).astype(ml_dtypes.bfloat16)
    ex = np.zeros((32, 16, 128), np.float32)
    for kt in range(16):
        for j in range(128):
            ex[2 * kt + j // 64, kt, j] = 1.0
    cb[0:32, 5120:7168] = ex.reshape(32, 2048).astype(ml_dtypes.bfloat16)
    s_ = np.arange(128)[:, None]
    t_ = np.arange(128)[None, :]
    cmask = (t_ >= s_).astype(np.int32)
    return dict(cf=cf, cbt=np.ascontiguousarray(cbt.reshape(2, 128, 3072)), cb=cb, cmask=cmask)


def _tile_w(w, ncol_tiles):
    a = w.reshape(NKC, 128, ncol_tiles, 128)
    return np.ascontiguousarray(a.transpose(2, 1, 0, 3)).reshape(ncol_tiles, 128, NKC * 128)


def prep_shared(inp):
    f = np.float32
    w_in = np.asarray(inp["w_in"], f)
    win = []
    for l in range(2):
        w = w_in[l]
        wp = np.concatenate([w[:, :5632], w[:, 5680:], w[:, 5632:5680], np.zeros((D, NPROJ - 11824), f)], axis=1)
        win.append(_tile_w(wp, NCT_IN))
    win = np.concatenate(win, axis=0)
    def _tile_ada(w):
        a = w.reshape(4, 8, 128, 24, 512)
        return np.ascontiguousarray(a.transpose(3, 0, 2, 1, 4)).reshape(96, 128, 4096)
    wada = np.concatenate([_tile_ada(np.asarray(inp["w_ada"][l], f)) for l in range(2)], axis=0)
    wout = np.concatenate([_tile_w(np.asarray(inp["w_out"][l], f), 32) for l in range(2)], axis=0)
    prm = np.zeros((128, 2, NPRM), f)

    def pt(v, n):
        return np.asarray(v, f).reshape(n, 128).T

    for l in range(2):
        cw = np.asarray(inp["rg_conv_w"][l], f)
        prm[:, l, P_CONVW:P_CONVW + 32] = cw.reshape(4, 8, 128).transpose(2, 1, 0).reshape(128, 32)
        prm[:, l, P_CONVB:P_CONVB + 8] = pt(inp["rg_conv_b"][l], 8)
        prm[:, l, P_BA:P_BA + 8] = pt(inp["rg_b_a"][l], 8)
        prm[:, l, P_BX:P_BX + 8] = pt(inp["rg_b_x"][l], 8)
        prm[:, l, P_LAM:P_LAM + 8] = pt(inp["rg_lambda"][l], 8)
        for l2 in range(2):
            prm[:, l, P_LB + l2 * 8:P_LB + l2 * 8 + 8] = pt(inp["hg_lower_bounds"][l2], 8)
        prm[:, l, P_NG] = np.asarray(inp["hg_norm_g"][l], f)
        prm[:, l, P_LNG:P_LNG + 32] = pt(inp["ln_g"][l], 32)
        prm[:, l, P_LNB:P_LNB + 32] = pt(inp["ln_b"][l], 32)
        prm[:, l, P_BADA:P_BADA + 96] = pt(inp["b_ada"][l], 96)
        prm[:, l, P_PEK:P_PEK + 32] = np.asarray(inp["nsa_pe_k"][l], f).T
        prm[:, l, P_PEV:P_PEV + 32] = np.asarray(inp["nsa_pe_v"][l], f).T
    rgw = np.zeros((2, 128, 2, 8, 128), f)
    cw1 = np.zeros((2, 2, 128, 32, 256), f)
    cw2 = np.zeros((2, 128, 2, 2, 128), f)
    for l in range(2):
        rgw[l, :, 0] = np.asarray(inp["rg_w_a"][l], f).transpose(1, 0, 2)
        rgw[l, :, 1] = np.asarray(inp["rg_w_x"][l], f).transpose(1, 0, 2)
        for xi, (k1, k2) in enumerate((("nsa_cmp_w1_k", "nsa_cmp_w2_k"), ("nsa_cmp_w1_v", "nsa_cmp_w2_v"))):
            cw1[l, xi] = np.asarray(inp[k1][l], f).reshape(32, 128, 256).transpose(1, 0, 2)
            cw2[l, :, xi] = np.asarray(inp[k2][l], f).reshape(2, 128, 128).transpose(1, 0, 2)
    sh = dict(wada=wada, win=win, wout=wout, prm=prm, rgw=rgw.reshape(2, 128, 2048),
              cw1=cw1.reshape(2, 2, 128, 8192), cw2=cw2.reshape(2, 128, 512))
    sh.update(make_consts())
    return sh


def prep_core(inp, b):
    x = np.asarray(inp["x"], np.float32)
    c = np.asarray(inp["c"], np.float32)
    xT = np.ascontiguousarray(x[b].T)
    cT = np.repeat(c[b].reshape(NKC, 128).T[:, :, None], 2, axis=2).reshape(128, 64)
    return dict(xT=xT, cT=np.ascontiguousarray(cT))


def build_program(**kw):
    nc = bass.Bass("TRN2", target_bir_lowering=False)
    p = Prog(nc, **kw)
    p.eps_ap = LN_EPS
    p.build()
    return nc


def kernel(**inputs):
    sh = prep_shared(inputs)
    nc = build_program()
    ACTIVE = (0, 1, 4, 5)
    zeros = None
    in_maps = []
    for core in range(8):
        if core in ACTIVE:
            m = dict(sh)
            m.update(prep_core(inputs, ACTIVE.index(core)))
        else:
            if zeros is None:
                zeros = {k: np.zeros_like(v) for k, v in in_maps[0].items()}
            m = zeros
        in_maps.append(m)
    res = run_bass_kernel_spmd(nc, in_maps, core_ids=list(range(8)))
    out = np.stack([np.asarray(res.results[c]["outT"], np.float32).T for c in ACTIVE], axis=0)
    return np.ascontiguousarray(out)
```
